# Optimizing a Trainium2 kernel written in Bass

```python
import jax, jax.numpy as jnp
from jax import lax
import numpy as np

D_MODEL = 1024
BATCH = 16
SEQ = 4096
DEPTH = 1

CHUNK = 64
Q_BLOCK = 128
EPS = 1e-6

SSD_HEAD_DIM = 64
SSD_INNER = D_MODEL
SSD_HEADS = SSD_INNER // SSD_HEAD_DIM
SSD_GROUPS = 2
SSD_HEADS_PER_GROUP = SSD_HEADS // SSD_GROUPS
SSD_STATE = 128
CONV_WIDTH = 4
CONV_DIM = SSD_INNER + 2 * SSD_GROUPS * SSD_STATE

FOX_HEAD_DIM = 64
FOX_WIDTH = D_MODEL
FOX_HEADS = FOX_WIDTH // FOX_HEAD_DIM

D_FF = 2816

IN_SIZES = (SSD_INNER, CONV_DIM, SSD_HEADS, FOX_WIDTH, FOX_WIDTH, FOX_WIDTH, FOX_HEADS, D_MODEL, D_MODEL)
W_IN_COLS = sum(IN_SIZES)
IN_SPLITS = tuple(int(sum(IN_SIZES[:i + 1])) for i in range(len(IN_SIZES) - 1))

kernel_name = "streaming_hybrid_ssd_fox_macaron"


def rms_norm(x, w):
    xf = x.astype(jnp.float32)
    y = xf * lax.rsqrt(jnp.mean(xf * xf, axis=-1, keepdims=True) + EPS)
    return (y * w.astype(jnp.float32)).astype(x.dtype)


def swiglu(h, w1, w3, w2):
    return (jax.nn.silu(h @ w1) * (h @ w3)) @ w2


def causal_dwconv(u, w, b):
    y = lax.conv_general_dilated(
        u, w, window_strides=(1,), padding=((CONV_WIDTH - 1, 0),),
        dimension_numbers=("NWC", "WIO", "NWC"), feature_group_count=u.shape[-1])
    return y + b


def ssd_chunked(x, dtA, Bm, Cm):
    a_cs = jnp.cumsum(dtA, axis=2)
    pos = jnp.arange(CHUNK)
    tril = (pos[:, None] >= pos[None, :])[None, None, :, :, None, None]
    seg = a_cs[:, :, :, None] - a_cs[:, :, None, :]
    L = jnp.exp(jnp.where(tril, seg, -jnp.inf))
    cb = jnp.einsum("bclgn,bcsgn->bclsg", Cm, Bm)
    y_diag = jnp.einsum("bclsg,bclsgr,bcsgrp->bclgrp", cb, L, x)
    decay_to_end = jnp.exp(a_cs[:, :, -1:] - a_cs)
    chunk_states = jnp.einsum("bclgn,bclgr,bclgrp->bcgrpn", Bm, decay_to_end, x)
    chunk_decay = jnp.exp(a_cs[:, :, -1])

    def step(h, inp):
        dec, st = inp
        return dec[..., None, None] * h + st, h

    b = x.shape[0]
    h0 = jnp.zeros((b, SSD_GROUPS, SSD_HEADS_PER_GROUP, SSD_HEAD_DIM, SSD_STATE), jnp.float32)
    _, h_in = lax.scan(step, h0, (jnp.moveaxis(chunk_decay, 1, 0), jnp.moveaxis(chunk_states, 1, 0)))
    h_in = jnp.moveaxis(h_in, 0, 1)
    y_off = jnp.einsum("bclgn,bcgrpn,bclgr->bclgrp", Cm, h_in, jnp.exp(a_cs))
    return y_diag + y_off


def ssd_mixer(z, xbc_raw, dt_raw, conv_w, conv_b, dt_bias, a_log, d_skip, norm_w):
    b, s, _ = z.shape
    nc = s // CHUNK
    f32 = jnp.float32
    xbc = jax.nn.silu(causal_dwconv(xbc_raw, conv_w, conv_b))
    xs, Bm, Cm = jnp.split(xbc, [SSD_INNER, SSD_INNER + SSD_GROUPS * SSD_STATE], axis=-1)
    xs = xs.astype(f32).reshape(b, nc, CHUNK, SSD_GROUPS, SSD_HEADS_PER_GROUP, SSD_HEAD_DIM)
    Bm = Bm.astype(f32).reshape(b, nc, CHUNK, SSD_GROUPS, SSD_STATE)
    Cm = Cm.astype(f32).reshape(b, nc, CHUNK, SSD_GROUPS, SSD_STATE)
    dt = jax.nn.softplus(dt_raw.astype(f32) + dt_bias.astype(f32))
    dt = dt.reshape(b, nc, CHUNK, SSD_GROUPS, SSD_HEADS_PER_GROUP)
    A = -jnp.exp(a_log.astype(f32)).reshape(SSD_GROUPS, SSD_HEADS_PER_GROUP)
    y = ssd_chunked(xs * dt[..., None], dt * A, Bm, Cm)
    y = y + d_skip.astype(f32).reshape(SSD_GROUPS, SSD_HEADS_PER_GROUP)[:, :, None] * xs
    y = y.reshape(b, s, SSD_INNER) * jax.nn.silu(z.astype(f32))
    yg = y.reshape(b, s, SSD_GROUPS, SSD_INNER // SSD_GROUPS)
    yg = yg * lax.rsqrt(jnp.mean(yg * yg, axis=-1, keepdims=True) + EPS)
    y = yg.reshape(b, s, SSD_INNER) * norm_w.astype(f32)
    return y.astype(z.dtype)


def fox_mixer(q, k, v, f_logit, b_f, q_norm_w, k_norm_w):
    b, s, _ = q.shape
    f32 = jnp.float32
    q = rms_norm(q.reshape(b, s, FOX_HEADS, FOX_HEAD_DIM), q_norm_w)
    k = rms_norm(k.reshape(b, s, FOX_HEADS, FOX_HEAD_DIM), k_norm_w)
    v = v.reshape(b, s, FOX_HEADS, FOX_HEAD_DIM)
    log_f = jax.nn.log_sigmoid(f_logit.astype(f32) + b_f.astype(f32))
    cum = jnp.transpose(jnp.cumsum(log_f, axis=1), (0, 2, 1))
    scale = FOX_HEAD_DIM ** -0.5
    outs = []
    for i in range(s // Q_BLOCK):
        q0 = i * Q_BLOCK
        kl = q0 + Q_BLOCK
        sc = jnp.einsum("bqhd,bkhd->bhqk", q[:, q0:kl], k[:, :kl]).astype(f32) * scale
        sc = sc + cum[:, :, q0:kl, None] - cum[:, :, None, :kl]
        mask = jnp.arange(kl)[None, :] <= (q0 + jnp.arange(Q_BLOCK))[:, None]
        sc = jnp.where(mask, sc, -jnp.inf)
        p = jax.nn.softmax(sc, axis=-1).astype(v.dtype)
        outs.append(jnp.einsum("bhqk,bkhd->bqhd", p, v[:, :kl]))
    return jnp.concatenate(outs, axis=1).reshape(b, s, FOX_WIDTH)


def setup_inputs(seed: int = 0) -> dict:
    key = jax.random.key(seed)
    ks = jax.random.split(key, 24)
    nrm = jax.random.normal
    uni = jax.random.uniform
    f32 = jnp.float32
    dt0 = jnp.exp(uni(ks[8], (DEPTH, SSD_HEADS), f32, np.log(1e-3), np.log(1e-1)))
    return {
        "x": nrm(ks[0], (BATCH, SEQ, D_MODEL), f32),
        "ffn1_norm": 1.0 + 0.02 * nrm(ks[1], (DEPTH, D_MODEL), f32),
        "ffn1_w1": nrm(ks[2], (DEPTH, D_MODEL, D_FF), f32) * D_MODEL ** -0.5,
        "ffn1_w3": nrm(ks[3], (DEPTH, D_MODEL, D_FF), f32) * D_MODEL ** -0.5,
        "ffn1_w2": nrm(ks[4], (DEPTH, D_FF, D_MODEL), f32) * D_FF ** -0.5,
        "mix_norm": 1.0 + 0.02 * nrm(ks[5], (DEPTH, D_MODEL), f32),
        "w_in": nrm(ks[6], (DEPTH, D_MODEL, W_IN_COLS), f32) * D_MODEL ** -0.5,
        "conv_w": nrm(ks[7], (DEPTH, CONV_WIDTH, 1, CONV_DIM), f32) * CONV_WIDTH ** -0.5,
        "conv_b": 0.02 * nrm(ks[9], (DEPTH, CONV_DIM), f32),
        "dt_bias": dt0 + jnp.log(-jnp.expm1(-dt0)),
        "a_log": jnp.log(uni(ks[10], (DEPTH, SSD_HEADS), f32, 1.0, 16.0)),
        "d_skip": 1.0 + 0.1 * nrm(ks[11], (DEPTH, SSD_HEADS), f32),
        "ssd_norm_w": 1.0 + 0.02 * nrm(ks[12], (DEPTH, SSD_INNER), f32),
        "fox_b_f": uni(ks[13], (DEPTH, FOX_HEADS), f32, 1.0, 4.0),
        "q_norm_w": 1.0 + 0.02 * nrm(ks[14], (DEPTH, FOX_HEAD_DIM), f32),
        "k_norm_w": 1.0 + 0.02 * nrm(ks[15], (DEPTH, FOX_HEAD_DIM), f32),
        "w_branch_ssd": nrm(ks[16], (DEPTH, SSD_INNER, D_MODEL), f32) * SSD_INNER ** -0.5,
        "w_branch_fox": nrm(ks[17], (DEPTH, FOX_WIDTH, D_MODEL), f32) * FOX_WIDTH ** -0.5,
        "w_out": nrm(ks[18], (DEPTH, D_MODEL, D_MODEL), f32) * D_MODEL ** -0.5,
        "ffn2_norm": 1.0 + 0.02 * nrm(ks[19], (DEPTH, D_MODEL), f32),
        "ffn2_w1": nrm(ks[20], (DEPTH, D_MODEL, D_FF), f32) * D_MODEL ** -0.5,
        "ffn2_w3": nrm(ks[21], (DEPTH, D_MODEL, D_FF), f32) * D_MODEL ** -0.5,
        "ffn2_w2": nrm(ks[22], (DEPTH, D_FF, D_MODEL), f32) * D_FF ** -0.5,
    }


def reference(x, ffn1_norm, ffn1_w1, ffn1_w3, ffn1_w2, mix_norm, w_in, conv_w, conv_b,
              dt_bias, a_log, d_skip, ssd_norm_w, fox_b_f, q_norm_w, k_norm_w,
              w_branch_ssd, w_branch_fox, w_out, ffn2_norm, ffn2_w1, ffn2_w3, ffn2_w2):
    for l in range(DEPTH):
        x = x + 0.5 * swiglu(rms_norm(x, ffn1_norm[l]), ffn1_w1[l], ffn1_w3[l], ffn1_w2[l])
        h = rms_norm(x, mix_norm[l])
        proj = h @ w_in[l]
        z, xbc, dt_raw, q, k, v, f_logit, g_ssd, g_fox = jnp.split(proj, IN_SPLITS, axis=-1)
        y_ssd = ssd_mixer(z, xbc, dt_raw, conv_w[l], conv_b[l], dt_bias[l], a_log[l],
                          d_skip[l], ssd_norm_w[l])
        y_fox = fox_mixer(q, k, v, f_logit, fox_b_f[l], q_norm_w[l], k_norm_w[l])
        merged = (jax.nn.sigmoid(g_ssd) * (y_ssd @ w_branch_ssd[l])
                  + jax.nn.sigmoid(g_fox) * (y_fox @ w_branch_fox[l]))
        x = x + merged @ w_out[l]
        x = x + 0.5 * swiglu(rms_norm(x, ffn2_norm[l]), ffn2_w1[l], ffn2_w3[l], ffn2_w2[l])
    return x
```

```python
import contextlib
import numpy as np
import concourse.bass as bass
import concourse.mybir as mybir
from concourse.bass_utils import run_bass_kernel_spmd

F32 = mybir.dt.float32
BF16 = mybir.dt.bfloat16
AF = mybir.ActivationFunctionType
ALU = mybir.AluOpType
AX = mybir.AxisListType

D = 1024
DFF = 2816
NFF = DFF // 128
EPS = 1e-6
NCORES = 8


class Buf:
    __slots__ = ("name", "writers", "readers", "multi")

    def __init__(self, name, multi=False):
        self.name = name
        self.writers = {}
        self.readers = {}
        self.multi = multi


class Op:
    __slots__ = ("eng", "fn", "needs", "own", "is_dma")


class Prog:
    ENGS = ("pe", "act", "dve", "pool", "sp")

    def __init__(self):
        self.streams = {e: [] for e in self.ENGS}
        self.count = {e: 0 for e in self.ENGS}
        self.waited = {e: {} for e in self.ENGS}
        self.signaled = {e: set() for e in self.ENGS}
        self.dma_count = {}
        self.key2phys = {}
        self.free_phys = []
        self.nphys = 0
        self.live = []
        self.barrier_needs = {e: {} for e in self.ENGS}

    def buf(self, name, multi=False, track=True):
        b = Buf(name, multi)
        if track:
            self.live.append(b)
        return b

    def add(self, eng, fn, reads=(), writes=(), dma_key=None):
        op = Op()
        op.eng = eng
        op.fn = fn
        op.is_dma = dma_key is not None
        if op.is_dma:
            phys = self.key2phys.get(dma_key)
            if phys is None:
                if self.free_phys:
                    phys = self.free_phys.pop()
                else:
                    phys = self.nphys
                    self.nphys += 1
                self.key2phys[dma_key] = phys
            k = ("dma", phys)
            self.dma_count[k] = self.dma_count.get(k, 0) + 16
            own = (k, self.dma_count[k])
        else:
            self.count[eng] += 1
            own = (eng, self.count[eng])
        op.own = own
        needs = {}

        def need(k, v):
            if k == "pe" and eng == "pe" and not op.is_dma:
                return
            if needs.get(k, 0) < v:
                needs[k] = v

        for k, v in self.barrier_needs[eng].items():
            need(k, v)
        self.barrier_needs[eng] = {}
        for b in reads:
            for k, v in b.writers.items():
                need(k, v)
        for b in writes:
            if not b.multi:
                for k, v in b.writers.items():
                    need(k, v)
            for k, v in b.readers.items():
                need(k, v)
        w = self.waited[eng]
        final = []
        for k, v in needs.items():
            if w.get(k, 0) >= v:
                continue
            w[k] = v
            final.append((k, v))
            if not isinstance(k, tuple):
                self.signaled[k].add(v)
        op.needs = final
        for b in reads:
            if b.readers.get(own[0], 0) < own[1]:
                b.readers[own[0]] = own[1]
        for b in writes:
            if b.multi:
                if b.writers.get(own[0], 0) < own[1]:
                    b.writers[own[0]] = own[1]
            else:
                b.writers = {own[0]: own[1]}
                b.readers = {}
        self.streams[eng].append(op)
        return op

    def barrier(self, drop=True):
        u = {}
        for b in self.live:
            for d in (b.writers, b.readers):
                for k, v in d.items():
                    if u.get(k, 0) < v:
                        u[k] = v
        for e in self.ENGS:
            bn = self.barrier_needs[e]
            for k, v in u.items():
                if bn.get(k, 0) < v:
                    bn[k] = v
        if drop:
            self.live = [b for b in self.live if b.multi == "dram"]
            self.free_phys.extend(sorted(set(self.key2phys.values()), reverse=True))
            self.key2phys = {}

    def emit(self, nc, stack):
        rank = {}
        for e in self.ENGS:
            rank[e] = {v: i + 1 for i, v in enumerate(sorted(self.signaled[e]))}
        sems = {}

        def sem(k):
            if k not in sems:
                nm = "s_" + (k if isinstance(k, str) else "d_" + str(k[1]))
                sems[k] = stack.enter_context(nc.semaphore(nm))
            return sems[k]

        for e in self.ENGS:
            sem(e)
        for k in self.dma_count:
            sem(k)
        block = stack.enter_context(nc.Block())
        engobj = {"pe": "tensor", "act": "scalar", "dve": "vector", "pool": "gpsimd", "sp": "sync"}
        nsem = len(sems)
        stats = {}
        for e in self.ENGS:
            ops = self.streams[e]
            stats[e] = len(ops)

            def body(eng, ops=ops, e=e):
                for op in ops:
                    for k, v in op.needs:
                        if isinstance(k, tuple):
                            eng.wait_ge(sems[k], v)
                        else:
                            eng.wait_ge(sems[k], rank[k][v])
                    if op.fn is None:
                        continue
                    ins = op.fn(eng)
                    if op.is_dma:
                        ins.then_inc(sems[op.own[0]], 16)
                    elif op.own[1] in rank[e]:
                        ins.then_inc(sems[e], 1)

            getattr(block, engobj[e])(body)
        return nsem, stats


class Builder:
    def __init__(self, S, stages="all", debug_outs=()):
        self.S = S
        self.T = 2 * S
        self.stages = stages
        self.debug_outs = debug_outs
        self.nc = bass.Bass("TRN2", target_bir_lowering=False)
        self.P = Prog()
        self.stack = contextlib.ExitStack()

    def dram_in(self, name, shape, dt=F32):
        return self.nc.dram_tensor(name, list(shape), dt, kind="ExternalInput").ap()

    def dram_out(self, name, shape, dt=F32):
        return self.nc.dram_tensor(name, list(shape), dt, kind="ExternalOutput").ap()

    def dram_scratch(self, name, shape, dt):
        if name in self.debug_outs:
            t = self.nc.dram_tensor(name, list(shape), dt, kind="ExternalOutput").ap()
        else:
            t = self.nc.dram_tensor(name, list(shape), dt).ap()
        b = self.P.buf(name, multi="dram")
        return t, b

    def sb(self, ctx, name, shape, dt):
        return ctx.enter_context(self.nc.sbuf_tensor(name, list(shape), dt))

    def build(self):
        nc, P = self.nc, self.P
        T = self.T
        with self.stack:
            st = self.stack
            self.x = self.dram_in("x", [T, D])
            self.x_buf = P.buf("x", multi="dram")
            shapes = {"ffn1_norm": [1, D], "ffn1_w1": [D, DFF], "ffn1_w3": [D, DFF], "ffn1_w2": [DFF, D],
                      "ffn2_norm": [1, D], "ffn2_w1": [D, DFF], "ffn2_w3": [D, DFF], "ffn2_w2": [DFF, D],
                      "mix_norm": [1, D], "w_in": [D, 7712], "conv_w": [4, 1536], "conv_b": [1, 1536],
                      "dt_bias": [1, 16], "a_log": [1, 16], "d_skip": [1, 16], "ssd_norm_w": [1, D],
                      "fox_b_f": [1, 16], "q_norm_w": [1, 64], "k_norm_w": [1, 64],
                      "w_branch_ssd": [D, D], "w_branch_fox": [D, D], "w_out": [D, D]}
            self.inp = {n: self.dram_in(n, shapes[n]) for n in shapes}
            S = self.S
            self.X1, self.X1_b = self.dram_scratch("X1", [T, D], F32)
            self.X2, self.X2_b = self.dram_scratch("X2", [T, D], F32)
            self.ZS, self.ZS_b = self.dram_scratch("ZS", [T, 1024], BF16)
            self.XSB, self.XSB_b = self.dram_scratch("XSB", [T, 1280], BF16)
            self.BCT, self.BCT_b = self.dram_scratch("BCT", [4, 128, T], BF16)
            self.DT, self.DT_b = self.dram_scratch("DT", [T, 16], F32)
            self.FT, self.FT_b = self.dram_scratch("FT", [16, T], F32)
            self.QKT, self.QKT_b = self.dram_scratch("QKT", [16, 128, T], BF16)
            self.V, self.V_b = self.dram_scratch("V", [T, 1024], BF16)
            self.CUM, self.CUM_b = self.dram_scratch("CUM", [2, 6, 16, S], BF16)
            self.YFT, self.YFT_b = self.dram_scratch("YFT", [1024, T], BF16)
            self.YS, self.YS_b = self.dram_scratch("YS", [T, 1024], BF16)
            self.HT, self.HT_b = self.dram_scratch("HT", [1024, T], BF16)
            self.wbuf = P.buf("weights_dram", multi="dram")
            self.out = self.dram_out("out", [T, D])
            self.out_buf = P.buf("out", multi="dram")

            self.ps = []
            self.psb = []
            for i in range(8):
                self.ps.append(st.enter_context(nc.psum_tensor(f"ps{i}", [128, 512], F32)))
                self.psb.append(P.buf(f"ps{i}", track=False))

            self.ident = st.enter_context(nc.sbuf_tensor("ident", [128, 128], BF16))
            self.ident_b = P.buf("ident", track=False)
            identf = st.enter_context(nc.sbuf_tensor("identf", [128, 128], F32))
            identf_b = P.buf("identf", track=False)
            self.identf, self.identf_b = identf, identf_b
            P.add("pool", lambda e: e.memset(identf[:], 0.0), writes=[identf_b])
            P.add("pool", lambda e: e.affine_select(identf[:], identf[:], [[-1, 128]], ALU.not_equal, 1.0,
                                                    base=0, channel_multiplier=1),
                  reads=[identf_b], writes=[identf_b])
            P.add("dve", lambda e: e.tensor_copy(self.ident[:], identf[:]), reads=[identf_b], writes=[self.ident_b])

            self.eps_t = st.enter_context(nc.sbuf_tensor("eps_t", [128, 1], F32))
            self.eps_b = P.buf("eps_t", track=False)
            P.add("pool", lambda e: e.memset(self.eps_t[:], EPS), writes=[self.eps_b])

            inp = self.inp
            if self.stages == "ffn1":
                self.ffn_phase("f1", self.x, self.x_buf, self.out, self.out_buf,
                               inp["ffn1_norm"], inp["ffn1_w1"], inp["ffn1_w3"], inp["ffn1_w2"])
            else:
                nst = {"p2": 2, "p3": 3, "p4": 4, "p5": 5, "all": 6}[self.stages]
                last = self.out if nst < 6 else None
                self.ffn_phase("f1", self.x, self.x_buf, self.X1, self.X1_b,
                               inp["ffn1_norm"], inp["ffn1_w1"], inp["ffn1_w3"], inp["ffn1_w2"])
                self.proj_phase()
                if nst >= 3:
                    self.attn_phase(with_ssd=(nst >= 4))
                if nst >= 5:
                    self.merge_phase()
                if nst >= 6:
                    self.ffn_phase("f2", self.X2, self.X2_b, self.out, self.out_buf,
                                   inp["ffn2_norm"], inp["ffn2_w1"], inp["ffn2_w3"], inp["ffn2_w2"])
                else:
                    P.add("sp", lambda e: e.dma_start(out=self.out[0:128, :], in_=self.X1[0:128, :]),
                          reads=[self.X1_b], writes=[self.out_buf], dma_key="dbgout")
            P.add("sp", None, reads=[self.out_buf])
            nsem, stats = P.emit(nc, st)
            self.info = (nsem, stats)
        return nc

    def fe_alloc(self, ctx, tag):
        P = self.P
        d = {}
        d["xs4"] = self.sb(ctx, tag + "xs4", [128, 4, D], F32)
        d["xs4_b"] = P.buf(tag + "xs4")
        d["hn4"] = self.sb(ctx, tag + "hn4", [128, 4, D], BF16)
        d["hn4_b"] = P.buf(tag + "hn4")
        d["ss4"] = self.sb(ctx, tag + "ss4", [128, 4], F32)
        d["ss4_b"] = P.buf(tag + "ss4")
        d["ln4"] = self.sb(ctx, tag + "ln4", [128, 4], F32)
        d["ln4_b"] = P.buf(tag + "ln4")
        d["rs4"] = self.sb(ctx, tag + "rs4", [128, 4], F32)
        d["rs4_b"] = P.buf(tag + "rs4")
        d["wn"] = self.sb(ctx, tag + "wn", [128, D], F32)
        d["wn_b"] = P.buf(tag + "wn")
        d["hT"] = self.sb(ctx, tag + "hT", [128, 8, 512], BF16)
        d["hT_b"] = P.buf(tag + "hT")
        return d

    def fe_a(self, d, src, src_b, r0):
        self.fe_load(d, src, src_b, r0)
        self.fe_stats(d)

    def fe_load(self, d, src, src_b, r0):
        P = self.P
        xs4 = d["xs4"]
        P.add("sp", lambda e: e.dma_start(out=xs4[:], in_=src[r0:r0 + 512, :].rearrange("(s p) d -> p s d", p=128)),
              reads=[src_b], writes=[d["xs4_b"]], dma_key=d["xs4_b"].name)

    def fe_stats(self, d):
        P = self.P
        xs4, hn4, ss4, ln4, rs4, wn = d["xs4"], d["hn4"], d["ss4"], d["ln4"], d["rs4"], d["wn"]
        for sub in range(4):
            P.add("act", lambda e, sub=sub: e.activation(out=hn4[:, sub, :], in_=xs4[:, sub, :], func=AF.Square,
                                                         accum_out=ss4[:, sub:sub + 1]),
                  reads=[d["xs4_b"]], writes=[d["hn4_b"], d["ss4_b"]])
        P.add("act", lambda e: e.activation(out=ln4[:], in_=ss4[:], func=AF.Ln, bias=self.eps_t[:, 0:1], scale=1.0 / D),
              reads=[d["ss4_b"], self.eps_b], writes=[d["ln4_b"]])
        P.add("act", lambda e: e.activation(out=rs4[:], in_=ln4[:], func=AF.Exp, scale=-0.5),
              reads=[d["ln4_b"]], writes=[d["rs4_b"]])
        for sub in range(4):
            P.add("dve", lambda e, sub=sub: e.scalar_tensor_tensor(
                out=hn4[:, sub, :], in0=xs4[:, sub, :], scalar=rs4[:, sub:sub + 1], in1=wn[:],
                op0=ALU.mult, op1=ALU.mult),
                reads=[d["xs4_b"], d["rs4_b"], d["wn_b"]], writes=[d["hn4_b"]])

    def fe_b(self, d, banks):
        P = self.P
        hn4, hT = d["hn4"], d["hT"]
        for sub in range(4):
            bi = banks[sub % 2]
            pt = self.ps[bi][:].bitcast(BF16)
            ptb = self.psb[bi]
            for k in range(8):
                P.add("pe", lambda e, k=k, sub=sub, pt=pt: e.transpose(
                    out=pt[:, k * 128:(k + 1) * 128], in_=hn4[:, sub, k * 128:(k + 1) * 128],
                    identity=self.ident[:]),
                    reads=[d["hn4_b"], self.ident_b], writes=[ptb])
            P.add("act", lambda e, sub=sub, pt=pt: e.copy(
                out=hT[:, :, sub * 128:(sub + 1) * 128], in_=pt.rearrange("p (k t) -> p k t", k=8)),
                reads=[ptb], writes=[d["hT_b"]])

    def ffn_phase(self, tag, src, src_b, dst, dst_b, norm_w, w1, w3, w2):
        nc, P = self.nc, self.P
        T = self.T
        NT = T // 512
        with contextlib.ExitStack() as ctx:
            w1b = self.sb(ctx, tag + "w1b", [128, 8, DFF], BF16)
            w3b = self.sb(ctx, tag + "w3b", [128, 8, DFF], BF16)
            w2b = self.sb(ctx, tag + "w2b", [128, NFF, D], BF16)
            WG = ((0, 4), (4, 10), (10, 16), (16, 22))
            w1_bs = [P.buf(f"{tag}w1b{g}", multi=True) for g in range(4)]
            w3_bs = [P.buf(f"{tag}w3b{g}", multi=True) for g in range(4)]
            gof = {c: gi_ for gi_, (a, b) in enumerate(WG) for c in range(a, b)}
            w2_b = P.buf(tag + "w2b", multi=True)
            fe = self.fe_alloc(ctx, tag)
            wn, wn_b = fe["wn"], fe["wn_b"]
            hT, hb = fe["hT"], fe["hT_b"]
            xr = [self.sb(ctx, f"{tag}xr{i}", [128, 512], F32) for i in range(2)]
            xr_b = [P.buf(f"{tag}xr{i}") for i in range(2)]
            g = self.sb(ctx, tag + "gT", [128, NFF, 512], BF16)
            gb = P.buf(tag + "gT")
            sl = [self.sb(ctx, f"{tag}sl{i}", [128, 512], F32) for i in range(2)]
            sl_b = [P.buf(f"{tag}sl{i}") for i in range(2)]

            w1v = w1.rearrange("(k p) f -> p k f", p=128)
            w3v = w3.rearrange("(k p) f -> p k f", p=128)
            w2v = w2.rearrange("(k p) f -> p k f", p=128)
            P.add("sp", lambda e: e.dma_start(out=wn[:], in_=norm_w.partition_broadcast(128)),
                  reads=[self.wbuf], writes=[wn_b], dma_key=wn_b.name)
            for wg_, (ca, cb_) in enumerate(WG):
                for k in range(8):
                    P.add("pool", lambda e, k=k, ca=ca, cb_=cb_: e.dma_start(
                        out=w1b[:, k, ca * 128:cb_ * 128], in_=w1v[:, k, ca * 128:cb_ * 128], max_dma_last_dim=4096),
                        reads=[self.wbuf], writes=[w1_bs[wg_]], dma_key=w1_bs[wg_].name)
                    P.add("pool", lambda e, k=k, ca=ca, cb_=cb_: e.dma_start(
                        out=w3b[:, k, ca * 128:cb_ * 128], in_=w3v[:, k, ca * 128:cb_ * 128], max_dma_last_dim=4096),
                        reads=[self.wbuf], writes=[w3_bs[wg_]], dma_key=w3_bs[wg_].name)
            for k in range(NFF):
                P.add("pool", lambda e, k=k: e.dma_start(out=w2b[:, k, :], in_=w2v[:, k, :], max_dma_last_dim=4096),
                      reads=[self.wbuf], writes=[w2_b], dma_key=w2_b.name)

            ps, psb = self.ps, self.psb
            nc_ = 0
            nr = 0
            self.fe_a(fe, src, src_b, 0)
            self.fe_b(fe, (6, 7))
            for it in range(NT):
                r0 = it * 512
                if it + 1 < NT:
                    self.fe_a(fe, src, src_b, r0 + 512)
                for c in range(NFF):
                    j = nc_ % 2
                    nc_ += 1
                    pa, pab = ps[2 * j], psb[2 * j]
                    pb, pbb = ps[2 * j + 1], psb[2 * j + 1]
                    for k in range(8):
                        P.add("pe", lambda e, k=k, c=c, pa=pa: e.matmul(
                            out=pa[:], lhsT=w1b[:, k, c * 128:(c + 1) * 128], rhs=hT[:, k, :],
                            start=(k == 0), stop=(k == 7)), reads=[w1_bs[gof[c]], hb],
                            writes=[pab, pbb] if k == 0 else [pab])
                    for k in range(8):
                        P.add("pe", lambda e, k=k, c=c, pb=pb: e.matmul(
                            out=pb[:], lhsT=w3b[:, k, c * 128:(c + 1) * 128], rhs=hT[:, k, :],
                            start=(k == 0), stop=(k == 7)), reads=[w3_bs[gof[c]], hb], writes=[pbb])
                    P.add("act", lambda e, pa=pa, j=j: e.activation(out=sl[j][:], in_=pa[:], func=AF.Silu),
                          reads=[pab], writes=[sl_b[j]])
                    P.add("dve", lambda e, pb=pb, j=j, c=c: e.tensor_tensor(
                        out=g[:, c, :], in0=sl[j][:], in1=pb[:], op=ALU.mult),
                        reads=[sl_b[j], pbb], writes=[gb])
                if it + 1 < NT:
                    self.fe_b(fe, (6, 7))
                for sub in range(4):
                    for half in range(2):
                        j = nr % 2
                        nr += 1
                        pc, pcb = ps[4 + j], psb[4 + j]
                        rr = r0 + sub * 128
                        P.add("sp", lambda e, j=j, rr=rr, half=half: e.dma_start(
                            out=xr[j][:], in_=src[rr:rr + 128, half * 512:(half + 1) * 512]),
                            reads=[src_b], writes=[xr_b[j]], dma_key=xr_b[j].name)
                        for c in range(NFF):
                            P.add("pe", lambda e, c=c, pc=pc, sub=sub, half=half: e.matmul(
                                out=pc[:], lhsT=g[:, c, sub * 128:(sub + 1) * 128],
                                rhs=w2b[:, c, half * 512:(half + 1) * 512],
                                start=(c == 0), stop=(c == NFF - 1)), reads=[w2_b, gb], writes=[pcb])
                        P.add("dve", lambda e, j=j, pc=pc: e.scalar_tensor_tensor(
                            out=xr[j][:], in0=pc[:], scalar=0.5, in1=xr[j][:], op0=ALU.mult, op1=ALU.add),
                            reads=[pcb, xr_b[j]], writes=[xr_b[j]])
                        P.add("sp", lambda e, j=j, rr=rr, half=half: e.dma_start(
                            out=dst[rr:rr + 128, half * 512:(half + 1) * 512], in_=xr[j][:]),
                            reads=[xr_b[j]], writes=[dst_b], dma_key=xr_b[j].name)
            P.barrier()


    def proj_phase(self):
        nc, P = self.nc, self.P
        T, S = self.T, self.S
        NT = T // 512
        NW = 5664
        tag = "p2"
        inp = self.inp
        ps, psb = self.ps, self.psb
        with contextlib.ExitStack() as ctx:
            wb = self.sb(ctx, tag + "win", [128, 8, NW], BF16)
            WR = ((1024, 2560), (0, 1024), (4624, 5664), (2560, 4624))
            wb_bs = [P.buf(f"{tag}win{g}", multi=True) for g in range(4)]

            def wbof(col):
                for g, (a, b) in enumerate(WR):
                    if a <= col < b:
                        return wb_bs[g]
                raise AssertionError(col)
            wv = inp["w_in"].rearrange("(k p) f -> p k f", p=128)
            fe = self.fe_alloc(ctx, tag)
            hT, hb = fe["hT"], fe["hT_b"]
            P.add("sp", lambda e: e.dma_start(out=fe["wn"][:], in_=inp["mix_norm"].partition_broadcast(128)),
                  reads=[self.wbuf], writes=[fe["wn_b"]], dma_key=fe["wn_b"].name)
            for g, (ca, cb_) in enumerate(WR):
                for k in range(8):
                    P.add("pool", lambda e, k=k, ca=ca, cb_=cb_: e.dma_start(
                        out=wb[:, k, ca:cb_], in_=wv[:, k, ca:cb_], max_dma_last_dim=4096),
                        reads=[self.wbuf], writes=[wb_bs[g]], dma_key=wb_bs[g].name)
            cw = self.sb(ctx, tag + "cw", [128, 12, 4], F32)
            cw_b = P.buf(tag + "cw", multi=True)
            cbias = self.sb(ctx, tag + "cbias", [128, 12], F32)
            cbias_b = P.buf(tag + "cbias", multi=True)
            for c in range(12):
                for j in range(4):
                    P.add("sp", lambda e, c=c, j=j: e.dma_start(
                        out=cw[:, c, j:j + 1], in_=inp["conv_w"][j:j + 1, c * 128:(c + 1) * 128].rearrange("o p -> p o")),
                        reads=[self.wbuf], writes=[cw_b], dma_key=cw_b.name)
                P.add("sp", lambda e, c=c: e.dma_start(
                    out=cbias[:, c:c + 1], in_=inp["conv_b"][0:1, c * 128:(c + 1) * 128].rearrange("o p -> p o")),
                    reads=[self.wbuf], writes=[cbias_b], dma_key=cbias_b.name)
            cbfull = self.sb(ctx, tag + "cbfull", [128, 1536], BF16)
            cbfull_b = P.buf(tag + "cbfull")
            P.add("pool", lambda e: e.memset(cbfull[:], 0.0), writes=[cbfull_b])
            P.add("pool", lambda e: e.dma_start(out=cbfull[0:1, :], in_=inp["conv_b"][0:1, :]),
                  reads=[self.wbuf, cbfull_b], writes=[cbfull_b], dma_key=cbfull_b.name)
            onesK = self.sb(ctx, tag + "onesK", [128, 128], BF16)
            onesK_b = P.buf(tag + "onesK")
            P.add("pool", lambda e: e.memset(onesK[:], 0.0), writes=[onesK_b])
            P.add("pool", lambda e: e.memset(onesK[0:1, :], 1.0), reads=[onesK_b], writes=[onesK_b])
            diagw = self.sb(ctx, tag + "diagw", [128, 12, 4, 128], BF16)
            diagw_b = P.buf(tag + "diagw")
            for c in range(12):
                for j in range(4):
                    P.add("dve", lambda e, c=c, j=j: e.tensor_scalar(
                        out=diagw[:, c, j, :], in0=self.identf[:], scalar1=cw[:, c, j:j + 1], scalar2=None,
                        op0=ALU.mult), reads=[self.identf_b, cw_b], writes=[diagw_b])
            wqk = self.sb(ctx, tag + "wqk", [128, 2], F32)
            wqk_b = P.buf(tag + "wqk", multi=True)
            for half in range(2):
                P.add("sp", lambda e, half=half: e.dma_start(
                    out=wqk[half * 64:(half + 1) * 64, 0:1], in_=inp["q_norm_w"].rearrange("o d -> d o")),
                    reads=[self.wbuf], writes=[wqk_b], dma_key=wqk_b.name)
                P.add("sp", lambda e, half=half: e.dma_start(
                    out=wqk[half * 64:(half + 1) * 64, 1:2], in_=inp["k_norm_w"].rearrange("o d -> d o")),
                    reads=[self.wbuf], writes=[wqk_b], dma_key=wqk_b.name)
            wqs = self.sb(ctx, tag + "wqs", [128, 2], F32)
            wqs_b = P.buf(tag + "wqs")
            P.add("dve", lambda e: e.tensor_copy(out=wqs[:], in_=wqk[:]), reads=[wqk_b], writes=[wqs_b])
            P.add("dve", lambda e: e.tensor_scalar(out=wqs[:, 0:1], in0=wqk[:, 0:1], scalar1=0.125, scalar2=None,
                                                   op0=ALU.mult), reads=[wqk_b, wqs_b], writes=[wqs_b])
            bd = self.sb(ctx, tag + "bd", [128, 128], BF16)
            bd_b = P.buf(tag + "bd")
            P.add("pool", lambda e: e.memset(bd[:], 0.0), writes=[bd_b])
            P.add("pool", lambda e: e.memset(bd[0:64, 0:64], 1.0), reads=[bd_b], writes=[bd_b])
            P.add("pool", lambda e: e.memset(bd[64:128, 64:128], 1.0), reads=[bd_b], writes=[bd_b])

            xbc = self.sb(ctx, tag + "xbc", [128, 12, 515], BF16)
            xbc_b = P.buf(tag + "xbc")
            csb = self.sb(ctx, tag + "csb", [128, 10, 512], BF16)
            csb_b = P.buf(tag + "csb")
            zst = [self.sb(ctx, f"{tag}zst{i}", [128, 1024], BF16) for i in range(2)]
            zst_b = [P.buf(f"{tag}zst{i}") for i in range(2)]
            xst = [self.sb(ctx, f"{tag}xst{i}", [128, 1280], BF16) for i in range(2)]
            xst_b = [P.buf(f"{tag}xst{i}") for i in range(2)]
            vst = [self.sb(ctx, f"{tag}vst{i}", [128, 1024], BF16) for i in range(2)]
            vst_b = [P.buf(f"{tag}vst{i}") for i in range(2)]
            dtst = self.sb(ctx, tag + "dtst", [128, 4, 16], F32)
            dtst_b = P.buf(tag + "dtst")
            fst = self.sb(ctx, tag + "fst", [16, 512], F32)
            fst_b = P.buf(tag + "fst")
            NR = 4
            fm = [self.sb(ctx, f"{tag}fm{i}", [128, 512], BF16) for i in range(NR)]
            fm_b = [P.buf(f"{tag}fm{i}") for i in range(NR)]
            sq = [self.sb(ctx, f"{tag}sq{i}", [128, 512], BF16) for i in range(2)]
            sq_b = [P.buf(f"{tag}sq{i}") for i in range(2)]
            lr = [self.sb(ctx, f"{tag}lr{i}", [128, 512], F32) for i in range(2)]
            lr_b = [P.buf(f"{tag}lr{i}") for i in range(2)]

            rot = [0]

            def nb():
                i = rot[0] % 6
                rot[0] += 1
                return ps[i], psb[i]

            cnt = {"fm": 0, "z": 0, "x": 0, "v": 0, "qk": 0}

            def proj_fm(col0, M):
                pa, pab = nb()
                for k in range(8):
                    P.add("pe", lambda e, k=k, pa=pa: e.matmul(
                        out=pa[0:M, :], lhsT=wb[:, k, col0:col0 + M], rhs=hT[:, k, :],
                        start=(k == 0), stop=(k == 7)), reads=[wbof(col0), hb], writes=[pab])
                return pa, pab

            def proj_tm(col0, N, sub, pa, pab, o0=0):
                for k in range(8):
                    P.add("pe", lambda e, k=k: e.matmul(
                        out=pa[:, o0:o0 + N], lhsT=hT[:, k, sub * 128:(sub + 1) * 128], rhs=wb[:, k, col0:col0 + N],
                        start=(k == 0), stop=(k == 7)), reads=[wbof(col0), hb], writes=[pab])

            self.fe_a(fe, self.X1, self.X1_b, 0)
            self.fe_b(fe, (6, 7))
            for it in range(NT):
                r0 = it * 512
                P.add("sp", lambda e, r0=r0: e.dma_start(
                    out=self.HT[:, r0:r0 + 512].rearrange("(k p) t -> p k t", p=128), in_=hT[:]),
                    reads=[hb], writes=[self.HT_b], dma_key=tag + "hTst")
                if it + 1 < NT:
                    self.fe_load(fe, self.X1, self.X1_b, r0 + 512)
                if r0 % S == 0:
                    P.add("pool", lambda e: e.memset(xbc[:, :, 0:3], 0.0), writes=[xbc_b])
                for c in range(12):
                    pa, pab = proj_fm(1024 + c * 128, 128)
                    P.add("act", lambda e, c=c, pa=pa: e.copy(out=xbc[:, c, 3:515], in_=pa[:]),
                          reads=[pab], writes=[xbc_b])
                for c in range(12):
                    pa, pab = nb()
                    for tap in range(4):
                        P.add("pe", lambda e, c=c, tap=tap, pa=pa: e.matmul(
                            out=pa[:], lhsT=diagw[:, c, tap, :], rhs=xbc[:, c, tap:tap + 512],
                            start=(tap == 0), stop=(tap == 3)), reads=[xbc_b, diagw_b], writes=[pab])
                    if c < 10:
                        P.add("act", lambda e, c=c, pa=pa: e.activation(
                            out=csb[:, c, :], in_=pa[:], func=AF.Silu, bias=cbias[:, c:c + 1]),
                            reads=[pab, cbias_b], writes=[csb_b])
                        if c >= 8:
                            P.add("sp", lambda e, c=c, r0=r0: e.dma_start(
                                out=self.BCT[c - 8, :, r0:r0 + 512], in_=csb[:, c, :]),
                                reads=[csb_b], writes=[self.BCT_b], dma_key=f"{tag}csb{c}")
                    else:
                        j = cnt["fm"] % NR
                        cnt["fm"] += 1
                        P.add("act", lambda e, c=c, pa=pa, j=j: e.activation(
                            out=fm[j][:], in_=pa[:], func=AF.Silu, bias=cbias[:, c:c + 1]),
                            reads=[pab, cbias_b], writes=[fm_b[j]])
                        P.add("sp", lambda e, c=c, j=j, r0=r0: e.dma_start(
                            out=self.BCT[c - 8, :, r0:r0 + 512], in_=fm[j][:]),
                            reads=[fm_b[j]], writes=[self.BCT_b], dma_key=fm_b[j].name)
                for sub in range(4):
                    j = cnt["x"] % 2
                    cnt["x"] += 1
                    pA, pAb = nb()
                    pB, pBb = nb()
                    pAv = pA[:].bitcast(BF16)
                    pBv = pB[:].bitcast(BF16)
                    for c in range(10):
                        dstv = pAv[:, c * 128:(c + 1) * 128] if c < 8 else pBv[:, (c - 8) * 128:(c - 7) * 128]
                        P.add("pe", lambda e, c=c, sub=sub, dstv=dstv: e.transpose(
                            out=dstv, in_=csb[:, c, sub * 128:(sub + 1) * 128], identity=self.ident[:]),
                            reads=[csb_b, self.ident_b], writes=[pAb if c < 8 else pBb])
                    P.add("act", lambda e, j=j, pAv=pAv: e.copy(out=xst[j][:, 0:1024], in_=pAv),
                          reads=[pAb], writes=[xst_b[j]])
                    P.add("dve", lambda e, j=j, pBv=pBv: e.tensor_copy(out=xst[j][:, 1024:1280], in_=pBv[:, 0:256]),
                          reads=[pBb], writes=[xst_b[j]])
                    rr = r0 + sub * 128
                    P.add("sp", lambda e, j=j, rr=rr: e.dma_start(out=self.XSB[rr:rr + 128, :], in_=xst[j][:]),
                          reads=[xst_b[j]], writes=[self.XSB_b], dma_key=xst_b[j].name)
                P.add("pool", lambda e: e.tensor_copy(out=xbc[:, :, 0:3], in_=xbc[:, :, 512:515]),
                      reads=[xbc_b], writes=[xbc_b])
                for sub in range(4):
                    rr = r0 + sub * 128
                    j = cnt["z"] % 2
                    cnt["z"] += 1
                    for half in range(2):
                        pa, pab = nb()
                        proj_tm(half * 512, 512, sub, pa, pab)
                        P.add("act", lambda e, pa=pa, j=j, half=half: e.activation(
                            out=zst[j][:, half * 512:(half + 1) * 512], in_=pa[:], func=AF.Silu),
                            reads=[pab], writes=[zst_b[j]])
                    P.add("sp", lambda e, j=j, rr=rr: e.dma_start(out=self.ZS[rr:rr + 128, :], in_=zst[j][:]),
                          reads=[zst_b[j]], writes=[self.ZS_b], dma_key=zst_b[j].name)
                    for half in range(2):
                        pa, pab = nb()
                        proj_tm(4624 + half * 512, 512, sub, pa, pab)
                        P.add("dve", lambda e, pa=pa, j=j, half=half: e.tensor_copy(
                            out=vst[j][:, half * 512:(half + 1) * 512], in_=pa[:]),
                            reads=[pab], writes=[vst_b[j]])
                    P.add("sp", lambda e, j=j, rr=rr: e.dma_start(out=self.V[rr:rr + 128, :], in_=vst[j][:]),
                          reads=[vst_b[j]], writes=[self.V_b], dma_key=vst_b[j].name)
                pa, pab = nb()
                for sub in range(4):
                    proj_tm(2560, 16, sub, pa, pab, o0=sub * 16)
                P.add("dve", lambda e, pa=pa: e.tensor_copy(
                    out=dtst[:], in_=pa[:, 0:64].rearrange("p (s h) -> p s h", s=4)),
                    reads=[pab], writes=[dtst_b])
                P.add("sp", lambda e, r0=r0: e.dma_start(
                    out=self.DT[r0:r0 + 512, :].rearrange("(s p) h -> p s h", p=128), in_=dtst[:]),
                    reads=[dtst_b], writes=[self.DT_b], dma_key=dtst_b.name)
                pa, pab = proj_fm(5648, 16)
                P.add("dve", lambda e, pa=pa: e.tensor_copy(out=fst[:], in_=pa[0:16, :]),
                      reads=[pab], writes=[fst_b])
                P.add("sp", lambda e, r0=r0: e.dma_start(out=self.FT[:, r0:r0 + 512], in_=fst[:]),
                      reads=[fst_b], writes=[self.FT_b], dma_key=fst_b.name)
                def qk_tail(ch, pa, pab, jq, r0=r0):
                    pb, pbb = nb()
                    P.add("pe", lambda e: e.matmul(out=pb[:], lhsT=bd[:], rhs=sq[jq][:], start=True, stop=True),
                          reads=[bd_b, sq_b[jq]], writes=[pbb])
                    P.add("act", lambda e: e.activation(
                        out=lr[jq][:], in_=pb[:], func=AF.Ln, bias=self.eps_t[:, 0:1], scale=1.0 / 64),
                        reads=[pbb, self.eps_b], writes=[lr_b[jq]])
                    P.add("act", lambda e: e.activation(out=lr[jq][:], in_=lr[jq][:], func=AF.Exp, scale=-0.5),
                          reads=[lr_b[jq]], writes=[lr_b[jq]])
                    j = cnt["fm"] % NR
                    cnt["fm"] += 1
                    wi = 0 if ch < 8 else 1
                    P.add("dve", lambda e: e.scalar_tensor_tensor(
                        out=fm[j][:], in0=pa[:], scalar=wqs[:, wi:wi + 1], in1=lr[jq][:],
                        op0=ALU.mult, op1=ALU.mult), reads=[pab, wqs_b, lr_b[jq]], writes=[fm_b[j]])
                    P.add("sp", lambda e: e.dma_start(out=self.QKT[ch, :, r0:r0 + 512], in_=fm[j][:]),
                          reads=[fm_b[j]], writes=[self.QKT_b], dma_key=fm_b[j].name)

                prev = None
                for ch in range(16):
                    if ch == 8 and it + 1 < NT:
                        self.fe_stats(fe)
                    pa, pab = proj_fm(2576 + ch * 128, 128)
                    jq = cnt["qk"] % 2
                    cnt["qk"] += 1
                    P.add("act", lambda e, pa=pa, jq=jq: e.activation(out=sq[jq][:], in_=pa[:], func=AF.Square),
                          reads=[pab], writes=[sq_b[jq]])
                    if prev is not None:
                        qk_tail(*prev)
                    prev = (ch, pa, pab, jq)
                qk_tail(*prev)
                if it + 1 < NT:
                    self.fe_b(fe, (6, 7))
            P.barrier()


    def attn_phase(self, with_ssd=False):
        nc, P = self.nc, self.P
        T, S = self.T, self.S
        NG = S // 512
        NKB = S // 128
        tag = "p3"
        inp = self.inp
        ps, psb = self.ps, self.psb
        with contextlib.ExitStack() as ctx:
            mk = self.sb(ctx, tag + "mk", [128, 128], BF16)
            mk_b = P.buf(tag + "mk")
            P.add("pool", lambda e: e.memset(mk[:], 0.0), writes=[mk_b])
            P.add("pool", lambda e: e.affine_select(mk[:], mk[:], [[1, 128]], ALU.is_ge, -30000.0, base=0,
                                                    channel_multiplier=-1), reads=[mk_b], writes=[mk_b])
            Sh = self.sb(ctx, tag + "Sh", [128, 128], F32)
            Sh_b = P.buf(tag + "Sh")
            P.add("pool", lambda e: e.memset(Sh[:], 0.0), writes=[Sh_b])
            P.add("pool", lambda e: e.affine_select(Sh[:], Sh[:], [[-1, 128]], ALU.not_equal, 1.0, base=-64,
                                                    channel_multiplier=1), reads=[Sh_b], writes=[Sh_b])
            R = self.sb(ctx, tag + "R", [128, 512], F32)
            R_b = P.buf(tag + "R")
            P.add("pool", lambda e: e.memset(R[:], 0.0), writes=[R_b])
            bfb = self.sb(ctx, tag + "bfb", [16, 1], F32)
            bfb_b = P.buf(tag + "bfb")
            P.add("sp", lambda e: e.dma_start(out=bfb[:], in_=inp["fox_b_f"].rearrange("o h -> h o")),
                  reads=[self.wbuf], writes=[bfb_b], dma_key=bfb_b.name)
            nbf = self.sb(ctx, tag + "nbf", [16, 1], F32)
            nbf_b = P.buf(tag + "nbf")
            P.add("dve", lambda e: e.tensor_scalar(out=nbf[:], in0=bfb[:], scalar1=-1.0, scalar2=None, op0=ALU.mult),
                  reads=[bfb_b], writes=[nbf_b])
            qp = [self.sb(ctx, f"{tag}qp{i}", [128, S], BF16) for i in range(2)]
            qp_b = [P.buf(f"{tag}qp{i}", multi=True) for i in range(2)]
            kp = [self.sb(ctx, f"{tag}kp{i}", [128, S], BF16) for i in range(2)]
            kp_b = [P.buf(f"{tag}kp{i}", multi=True) for i in range(2)]
            vp = [self.sb(ctx, f"{tag}vp{i}", [128, NKB, 128], BF16) for i in range(2)]
            vp_b = [P.buf(f"{tag}vp{i}", multi=True) for i in range(2)]
            for i in range(2):
                P.add("pool", lambda e, i=i: e.memset(qp[i][64:70, :], 1.0), writes=[qp_b[i]])
                P.add("pool", lambda e, i=i: e.memset(kp[i][64:70, :], 1.0), writes=[kp_b[i]])
                P.add("pool", lambda e, i=i: e.memset(vp[i][:], 1.0), writes=[vp_b[i]])
            pt = [self.sb(ctx, f"{tag}pt{i}", [128, 512], BF16) for i in range(4)]
            pt_b = [P.buf(f"{tag}pt{i}") for i in range(4)]
            osb = [self.sb(ctx, f"{tag}osb{i}", [64, 512], F32) for i in range(2)]
            osb_b = [P.buf(f"{tag}osb{i}") for i in range(2)]
            yst = [self.sb(ctx, f"{tag}yst{i}", [64, 512], BF16) for i in range(2)]
            yst_b = [P.buf(f"{tag}yst{i}") for i in range(2)]

            with contextlib.ExitStack() as c2:
                fsb = self.sb(c2, tag + "fsb", [16, S], F32)
                fsb_b = P.buf(tag + "fsb")
                cum = self.sb(c2, tag + "cum", [16, S], F32)
                cum_b = P.buf(tag + "cum")
                rr_ = self.sb(c2, tag + "rr", [16, S], F32)
                rr_b = P.buf(tag + "rr")
                one16 = self.sb(c2, tag + "one16", [16, S], BF16)
                one16_b = P.buf(tag + "one16")
                cp = self.sb(c2, tag + "cp", [16, 6, S], BF16)
                cp_b = P.buf(tag + "cp")
                P.add("pool", lambda e: e.memset(one16[:], 1.0), writes=[one16_b])
                for seq in range(2):
                    P.add("sp", lambda e, seq=seq: e.dma_start(out=fsb[:], in_=self.FT[:, seq * S:(seq + 1) * S]),
                          reads=[self.FT_b], writes=[fsb_b], dma_key=fsb_b.name)
                    P.add("act", lambda e: e.activation(out=fsb[:], in_=fsb[:], func=AF.Exp, bias=nbf[:, 0:1], scale=-1.0),
                          reads=[fsb_b, nbf_b], writes=[fsb_b])
                    P.add("act", lambda e: e.activation(out=fsb[:], in_=fsb[:], func=AF.Ln, bias=1.0),
                          reads=[fsb_b], writes=[fsb_b])
                    P.add("dve", lambda e: e.tensor_tensor_scan(out=cum[:], data0=one16[:], data1=fsb[:], initial=0.0,
                                                                op0=ALU.mult, op1=ALU.subtract),
                          reads=[one16_b, fsb_b], writes=[cum_b])
                    P.add("dve", lambda e: e.tensor_copy(out=cp[:, 0, :], in_=cum[:]), reads=[cum_b], writes=[cp_b])
                    P.add("dve", lambda e: e.tensor_tensor(out=rr_[:], in0=cum[:], in1=cp[:, 0, :], op=ALU.subtract),
                          reads=[cum_b, cp_b], writes=[rr_b])
                    P.add("dve", lambda e: e.tensor_copy(out=cp[:, 1, :], in_=rr_[:]), reads=[rr_b, cp_b], writes=[cp_b])
                    P.add("dve", lambda e: e.tensor_tensor(out=cum[:], in0=rr_[:], in1=cp[:, 1, :], op=ALU.subtract),
                          reads=[rr_b, cp_b], writes=[cum_b])
                    P.add("dve", lambda e: e.tensor_copy(out=cp[:, 2, :], in_=cum[:]), reads=[cum_b, cp_b], writes=[cp_b])
                    P.add("dve", lambda e: e.tensor_scalar(out=cp[:, 3:6, :], in0=cp[:, 0:3, :], scalar1=-1.0, scalar2=None,
                                                           op0=ALU.mult), reads=[cp_b], writes=[cp_b])
                    P.add("sp", lambda e, seq=seq: e.dma_start(out=self.CUM[seq].rearrange("j h s -> h j s"), in_=cp[:]),
                          reads=[cp_b], writes=[self.CUM_b], dma_key=cp_b.name)
                P.barrier(drop=False)

            def load(seq, h, j):
                c0 = seq * S
                hp, ho = h // 2, (h % 2) * 64
                P.add("sp", lambda e: e.dma_start(out=qp[j][0:64, :], in_=self.QKT[hp, ho:ho + 64, c0:c0 + S]),
                      reads=[self.QKT_b], writes=[qp_b[j]], dma_key=qp_b[j].name)
                P.add("sp", lambda e: e.dma_start(out=qp[j][64:67, :], in_=self.CUM[seq, 0:3, h, :]),
                      reads=[self.CUM_b], writes=[qp_b[j]], dma_key=qp_b[j].name)
                P.add("sp", lambda e: e.dma_start(out=kp[j][0:64, :], in_=self.QKT[8 + hp, ho:ho + 64, c0:c0 + S]),
                      reads=[self.QKT_b], writes=[kp_b[j]], dma_key=kp_b[j].name)
                P.add("sp", lambda e: e.dma_start(out=kp[j][67:70, :], in_=self.CUM[seq, 3:6, h, :]),
                      reads=[self.CUM_b], writes=[kp_b[j]], dma_key=kp_b[j].name)
                P.add("sp", lambda e: e.dma_start(
                    out=vp[j][:, :, 0:64],
                    in_=self.V[c0:c0 + S, h * 64:(h + 1) * 64].rearrange("(kb p) d -> p kb d", p=128)),
                    reads=[self.V_b], writes=[vp_b[j]], dma_key=vp_b[j].name)

            heads = [(seq, h) for seq in range(2) for h in range(16)]
            steps = []
            for hi, (seq, h) in enumerate(heads):
                for G in range(NG):
                    nkb = 4 * (G + 1)
                    for kb in range(nkb):
                        steps.append((hi, seq, h, G, kb, nkb))
            LA = 3
            LC = 6
            Rr = [R, self.sb(ctx, tag + "R1", [128, 512], F32)]
            Rr_b = [R_b, P.buf(tag + "R1")]
            P.add("pool", lambda e: e.memset(Rr[1][:], 0.0), writes=[Rr_b[1]])

            def stageA(i):
                hi, seq, h, G, kb, nkb = steps[i]
                j = hi % 2
                if G == 0 and kb == 3 and hi + 1 < len(heads):
                    load(heads[hi + 1][0], heads[hi + 1][1], (hi + 1) % 2)
                d = kb - 4 * G
                c0 = d * 128 if d > 0 else 0
                r = i % 4
                pS, pSb = ps[i % 3], psb[i % 3]
                P.add("pe", lambda e: e.matmul(
                    out=pS[:, c0:512], lhsT=kp[j][0:70, kb * 128:(kb + 1) * 128],
                    rhs=qp[j][0:70, G * 512 + c0:(G + 1) * 512], start=True, stop=(d < 0)),
                    reads=[kp_b[j], qp_b[j]] + ([pt_b[(i - LA) % 4]] if i - LA >= 0 else []), writes=[pSb])
                if d >= 0:
                    P.add("pe", lambda e: e.matmul(
                        out=pS[:, d * 128:(d + 1) * 128], lhsT=self.ident[:], rhs=mk[:],
                        start=False, stop=True), reads=[self.ident_b, mk_b], writes=[pSb])
                P.add("act", lambda e: e.activation(out=pt[r][:, c0:512], in_=pS[:, c0:512], func=AF.Exp),
                      reads=[pSb], writes=[pt_b[r]])

            gcount = [0]
            pending = []

            def stageB(i):
                hi, seq, h, G, kb, nkb = steps[i]
                j = hi % 2
                d = kb - 4 * G
                c0 = d * 128 if d > 0 else 0
                r = i % 4
                gi = gcount[0]
                jo = gi % 2
                po, pob = ps[3 + jo], psb[3 + jo]
                P.add("pe", lambda e: e.matmul(
                    out=po[:, c0:512], lhsT=vp[j][:, kb, :], rhs=pt[r][:, c0:512],
                    start=(kb == 0), stop=(kb == nkb - 1)), reads=[vp_b[j], pt_b[r]], writes=[pob])
                if kb == nkb - 1:
                    gcount[0] += 1
                    P.add("dve", lambda e: e.tensor_copy(out=osb[jo][:], in_=po[0:64, :]),
                          reads=[pob], writes=[osb_b[jo]])
                    P.add("dve", lambda e: e.reciprocal(out=Rr[jo][64:128, :], in_=po[64:128, :]),
                          reads=[pob, Rr_b[jo]], writes=[Rr_b[jo]])
                    pending.append((i + LC, jo, h, seq * S + G * 512))

            def stageC(jo, h, cc):
                P.add("pe", lambda e: e.matmul(out=ps[5][:], lhsT=Sh[:], rhs=Rr[jo][:], start=True, stop=True),
                      reads=[Sh_b, Rr_b[jo]], writes=[psb[5]])
                P.add("dve", lambda e: e.tensor_tensor(out=yst[jo][:], in0=osb[jo][:], in1=ps[5][0:64, :],
                                                       op=ALU.mult),
                      reads=[osb_b[jo], psb[5]], writes=[yst_b[jo]])
                P.add("sp", lambda e: e.dma_start(out=self.YFT[h * 64:(h + 1) * 64, cc:cc + 512], in_=yst[jo][:]),
                      reads=[yst_b[jo]], writes=[self.YFT_b], dma_key=yst_b[jo].name)

            ssd = self.ssd_setup(ctx, 6, 7) if with_ssd else None
            load(heads[0][0], heads[0][1], 0)
            N = len(steps)
            for i in range(N + LA):
                if i < N:
                    stageA(i)
                if i - LA >= 0:
                    stageB(i - LA)
                while pending and pending[0][0] <= i:
                    _, jo, h, cc = pending.pop(0)
                    stageC(jo, h, cc)
                if ssd is not None and (i % 2 == 1 or i % 16 == 0):
                    if next(ssd, "done") == "done":
                        ssd = None
            while pending:
                _, jo, h, cc = pending.pop(0)
                stageC(jo, h, cc)
            if ssd is not None:
                for _ in ssd:
                    pass
            P.barrier()

    def ssd_setup(self, ctx, bx, by):
        nc, P = self.nc, self.P
        T, S = self.T, self.S
        tag = "p4"
        inp = self.inp
        ps, psb = self.ps, self.psb
        X, Xb, Y, Yb = ps[bx], psb[bx], ps[by], psb[by]

        def const(name, shape, dt=F32):
            return self.sb(ctx, tag + name, shape, dt), P.buf(tag + name)

        U, U_b = const("U", [128, 128])
        Tm, Tm_b = const("Tm", [128, 128])
        on, on_b = const("ones", [128, 128])
        P.add("pool", lambda e: e.memset(U[:], 1.0), writes=[U_b])
        P.add("pool", lambda e: e.affine_select(U[:], U[:], [[1, 128]], ALU.is_ge, 0.0, base=0,
                                                channel_multiplier=-1), reads=[U_b], writes=[U_b])
        P.add("pool", lambda e: e.memset(Tm[:], 1.0), writes=[Tm_b])
        P.add("pool", lambda e: e.affine_select(Tm[:], Tm[:], [[-1, 128]], ALU.is_gt, 0.0, base=0,
                                                channel_multiplier=1), reads=[Tm_b], writes=[Tm_b])
        P.add("pool", lambda e: e.memset(on[:], 1.0), writes=[on_b])
        dtb, dtb_b = const("dtb", [128, 16])
        At, At_b = const("At", [128, 16])
        Dsk, Dsk_b = const("Dsk", [128, 16])
        nw, nw_b = const("nw", [128, 1024])
        P.add("sp", lambda e: e.dma_start(out=dtb[:], in_=inp["dt_bias"].partition_broadcast(128)),
              reads=[self.wbuf], writes=[dtb_b], dma_key=dtb_b.name)
        P.add("sp", lambda e: e.dma_start(out=At[:], in_=inp["a_log"].partition_broadcast(128)),
              reads=[self.wbuf], writes=[At_b], dma_key=At_b.name)
        P.add("sp", lambda e: e.dma_start(out=Dsk[:], in_=inp["d_skip"].partition_broadcast(128)),
              reads=[self.wbuf], writes=[Dsk_b], dma_key=Dsk_b.name)
        P.add("sp", lambda e: e.dma_start(out=nw[:], in_=inp["ssd_norm_w"].partition_broadcast(128)),
              reads=[self.wbuf], writes=[nw_b], dma_key=nw_b.name)
        P.add("act", lambda e: e.activation(out=At[:], in_=At[:], func=AF.Exp), reads=[At_b], writes=[At_b])
        P.add("dve", lambda e: e.tensor_scalar(out=At[:], in0=At[:], scalar1=-1.0, scalar2=None, op0=ALU.mult),
              reads=[At_b], writes=[At_b])
        S32, S32_b = const("S32", [128, 2, 512])
        Sbf, Sbf_b = const("Sbf", [128, 2, 512], BF16)

        def dbl(name, shape, dt=F32):
            return ([self.sb(ctx, f"{tag}{name}{i}", shape, dt) for i in range(2)],
                    [P.buf(f"{tag}{name}{i}") for i in range(2)])

        xsb, xsb_b = dbl("xsb", [128, 1280], BF16)
        bct, bct_b = dbl("bct", [128, 4, 128], BF16)
        dtr, dtr_b = dbl("dtr", [128, 16])
        zs, zs_b = dbl("zs", [128, 1024], BF16)
        ystg, ystg_b = dbl("ystg", [128, 1024], BF16)
        dtv, dtv_b = dbl("dtv", [128, 16])
        av, av_b = dbl("av", [128, 16])
        cs, cs_b = dbl("cs", [128, 32])
        ex, ex_b = dbl("ex", [128, 32])
        dte, dte_b = dbl("dte", [128, 16])
        rhsA, rhsA_b = dbl("rhsA", [128, 8, 128])
        L, L_b = dbl("L", [128, 8, 128], BF16)
        cbm, cbm_b = dbl("cbm", [128, 128], BF16)
        M, M_b = dbl("M", [128, 8, 128], BF16)
        xdt, xdt_b = dbl("xdt", [128, 8, 64], BF16)
        xdd, xdd_b = dbl("xdd", [128, 8, 64], BF16)
        t1, t1_b = dbl("t1", [128, 512])
        t2, t2_b = dbl("t2", [128, 512])
        junk, junk_b = const("junk", [128, 512], BF16)
        ssy, ssy_b = dbl("ssy", [128, 1])
        rs, rs_b = dbl("rs", [128, 1])

        def load(tl, j):
            r0 = tl * 128
            P.add("sp", lambda e: e.dma_start(out=xsb[j][:], in_=self.XSB[r0:r0 + 128, :]),
                  reads=[self.XSB_b], writes=[xsb_b[j]], dma_key=xsb_b[j].name)
            P.add("sp", lambda e: e.dma_start(out=bct[j][:], in_=self.BCT[:, :, r0:r0 + 128].rearrange("c p t -> p c t")),
                  reads=[self.BCT_b], writes=[bct_b[j]], dma_key=bct_b[j].name)
            P.add("sp", lambda e: e.dma_start(out=dtr[j][:], in_=self.DT[r0:r0 + 128, :]),
                  reads=[self.DT_b], writes=[dtr_b[j]], dma_key=dtr_b[j].name)
            P.add("sp", lambda e: e.dma_start(out=zs[j][:], in_=self.ZS[r0:r0 + 128, :]),
                  reads=[self.ZS_b], writes=[zs_b[j]], dma_key=zs_b[j].name)

        def bc_h(ap16, g):
            return ap16[:, 8 * g:8 * g + 8].unsqueeze(2)

        NTL = T // 128

        def tile(tl):
            j = tl % 2
            r0 = tl * 128
            if tl + 1 < NTL:
                load(tl + 1, (tl + 1) % 2)
            if r0 % S == 0:
                P.add("pool", lambda e: e.memset(S32[:], 0.0), writes=[S32_b])
                P.add("pool", lambda e: e.memset(Sbf[:], 0.0), writes=[Sbf_b])
            P.add("dve", lambda e: e.tensor_tensor(out=dtv[j][:], in0=dtr[j][:], in1=dtb[:], op=ALU.add),
                  reads=[dtr_b[j], dtb_b], writes=[dtv_b[j]])
            yield
            P.add("act", lambda e: e.activation(out=dtv[j][:], in_=dtv[j][:], func=AF.Exp),
                  reads=[dtv_b[j]], writes=[dtv_b[j]])
            P.add("act", lambda e: e.activation(out=dtv[j][:], in_=dtv[j][:], func=AF.Ln, bias=1.0),
                  reads=[dtv_b[j]], writes=[dtv_b[j]])
            yield
            P.add("dve", lambda e: e.tensor_tensor(out=av[j][:], in0=dtv[j][:], in1=At[:], op=ALU.mult),
                  reads=[dtv_b[j], At_b], writes=[av_b[j]])
            yield
            P.add("pe", lambda e: e.matmul(out=X[:, 0:16], lhsT=U[:], rhs=av[j][:], start=True, stop=True),
                  reads=[U_b, av_b[j]], writes=[Xb])
            P.add("pe", lambda e: e.matmul(out=X[:, 16:32], lhsT=on[:], rhs=av[j][:], start=True, stop=True),
                  reads=[on_b, av_b[j]], writes=[Xb])
            yield
            P.add("dve", lambda e: e.tensor_copy(out=cs[j][:], in_=X[:, 0:32]), reads=[Xb], writes=[cs_b[j]])
            yield
            P.add("act", lambda e: e.activation(out=ex[j][:], in_=cs[j][:], func=AF.Exp),
                  reads=[cs_b[j]], writes=[ex_b[j]])
            P.add("dve", lambda e: e.tensor_tensor(out=dte[j][:], in0=cs[j][:, 16:32], in1=cs[j][:, 0:16],
                                                   op=ALU.subtract), reads=[cs_b[j]], writes=[dte_b[j]])
            yield
            P.add("act", lambda e: e.activation(out=dte[j][:], in_=dte[j][:], func=AF.Exp),
                  reads=[dte_b[j]], writes=[dte_b[j]])
            for g in range(2):
                gi = g
                xs_g = xsb[j][:, g * 512:(g + 1) * 512].rearrange("p (h d) -> p h d", h=8)
                P.add("dve", lambda e, g=g, gi=gi: e.tensor_tensor(
                    out=rhsA[gi][:], in0=bc_h(av[j], g).broadcast_to([128, 8, 128]),
                    in1=U[:].unsqueeze(1).broadcast_to([128, 8, 128]), op=ALU.mult),
                    reads=[av_b[j], U_b], writes=[rhsA_b[gi]])
                P.add("pool", lambda e, g=g, gi=gi, xs_g=xs_g: e.tensor_tensor(
                    out=xdt[gi][:], in0=xs_g, in1=bc_h(dtv[j], g).broadcast_to([128, 8, 64]), op=ALU.mult),
                    reads=[xsb_b[j], dtv_b[j]], writes=[xdt_b[gi]])
                yield
                for hf, (Pb, Pbb) in enumerate(((X, Xb), (Y, Yb))):
                    P.add("pe", lambda e, hf=hf, gi=gi, Pb=Pb: e.matmul(
                        out=Pb[:], lhsT=Tm[:],
                        rhs=rhsA[gi][:, 4 * hf:4 * hf + 4, :].rearrange("p h l -> p (h l)"),
                        start=True, stop=True), reads=[Tm_b, rhsA_b[gi]], writes=[Pbb])
                yield
                for hf, (Pb, Pbb) in enumerate(((X, Xb), (Y, Yb))):
                    P.add("act", lambda e, hf=hf, gi=gi, Pb=Pb: e.activation(
                        out=L[gi][:, 4 * hf:4 * hf + 4, :].rearrange("p h l -> p (h l)"), in_=Pb[:],
                        func=AF.Exp), reads=[Pbb], writes=[L_b[gi]])
                P.add("pool", lambda e, g=g, gi=gi: e.tensor_tensor(
                    out=xdd[gi][:], in0=xdt[gi][:], in1=bc_h(dte[j], g).broadcast_to([128, 8, 64]), op=ALU.mult),
                    reads=[xdt_b[gi], dte_b[j]], writes=[xdd_b[gi]])
                yield
                P.add("pe", lambda e, g=g: e.matmul(out=X[:, 0:128], lhsT=bct[j][:, g, :], rhs=bct[j][:, 2 + g, :],
                                                   start=True, stop=True), reads=[bct_b[j]], writes=[Xb])
                yield
                P.add("dve", lambda e, gi=gi: e.tensor_tensor(out=cbm[gi][:], in0=X[:, 0:128], in1=U[:], op=ALU.mult),
                      reads=[Xb, U_b], writes=[cbm_b[gi]])
                yield
                P.add("dve", lambda e, gi=gi: e.tensor_tensor(
                    out=M[gi][:], in0=L[gi][:], in1=cbm[gi][:].unsqueeze(1).broadcast_to([128, 8, 128]),
                    op=ALU.mult), reads=[L_b[gi], cbm_b[gi]], writes=[M_b[gi]])
                P.add("pool", lambda e, g=g, gi=gi, xs_g=xs_g: e.tensor_tensor(
                    out=t2[gi][:].rearrange("p (h d) -> p h d", h=8), in0=xs_g,
                    in1=bc_h(Dsk, g).broadcast_to([128, 8, 64]), op=ALU.mult),
                    reads=[xsb_b[j], Dsk_b], writes=[t2_b[gi]])
                yield
                for h in range(8):
                    P.add("pe", lambda e, h=h, gi=gi: e.matmul(
                        out=X[:, h * 64:(h + 1) * 64], lhsT=M[gi][:, h, :], rhs=xdt[gi][:, h, :],
                        start=True, stop=True), reads=[M_b[gi], xdt_b[gi]], writes=[Xb])
                P.add("pe", lambda e, g=g: e.matmul(out=Y[:], lhsT=bct[j][:, 2 + g, :], rhs=Sbf[:, g, :],
                                                   start=True, stop=True),
                      reads=[bct_b[j], Sbf_b], writes=[Yb])
                yield
                P.add("dve", lambda e, g=g, gi=gi: e.tensor_tensor(
                    out=t1[gi][:].rearrange("p (h d) -> p h d", h=8), in0=Y[:].rearrange("p (h d) -> p h d", h=8),
                    in1=bc_h(ex[j], g).broadcast_to([128, 8, 64]), op=ALU.mult),
                    reads=[Yb, ex_b[j]], writes=[t1_b[gi]])
                yield
                P.add("dve", lambda e, gi=gi: e.tensor_tensor(out=t1[gi][:], in0=t1[gi][:], in1=X[:], op=ALU.add),
                      reads=[t1_b[gi], Xb], writes=[t1_b[gi]])
                yield
                P.add("pe", lambda e, g=g, gi=gi: e.matmul(
                    out=X[:], lhsT=xsb[j][:, 1024 + g * 128:1024 + (g + 1) * 128],
                    rhs=xdd[gi][:].rearrange("p h d -> p (h d)"), start=True, stop=True),
                    reads=[xsb_b[j], xdd_b[gi]], writes=[Xb])
                P.add("dve", lambda e, gi=gi: e.tensor_tensor(out=t1[gi][:], in0=t1[gi][:], in1=t2[gi][:], op=ALU.add),
                      reads=[t1_b[gi], t2_b[gi]], writes=[t1_b[gi]])
                S32g = S32[:, g, :].rearrange("p (h d) -> p h d", h=8)
                P.add("pool", lambda e, g=g, S32g=S32g: e.tensor_tensor(
                    out=S32g, in0=S32g, in1=ex[j][:, 16 + 8 * g:16 + 8 * g + 8].unsqueeze(2).broadcast_to([128, 8, 64]),
                    op=ALU.mult), reads=[S32_b, ex_b[j]], writes=[S32_b])
                yield
                P.add("dve", lambda e, g=g, gi=gi: e.tensor_tensor(
                    out=t1[gi][:], in0=t1[gi][:], in1=zs[j][:, g * 512:(g + 1) * 512], op=ALU.mult),
                    reads=[t1_b[gi], zs_b[j]], writes=[t1_b[gi]])
                yield
                P.add("act", lambda e, gi=gi: e.activation(out=junk[:], in_=t1[gi][:], func=AF.Square,
                                                          accum_out=ssy[gi][:]),
                      reads=[t1_b[gi]], writes=[junk_b, ssy_b[gi]])
                P.add("dve", lambda e, g=g: e.tensor_tensor(out=S32[:, g, :], in0=S32[:, g, :], in1=X[:], op=ALU.add),
                      reads=[S32_b, Xb], writes=[S32_b])
                yield
                P.add("act", lambda e, gi=gi: e.activation(out=rs[gi][:], in_=ssy[gi][:], func=AF.Ln,
                                                          bias=self.eps_t[:, 0:1], scale=1.0 / 512),
                      reads=[ssy_b[gi], self.eps_b], writes=[rs_b[gi]])
                P.add("act", lambda e, g=g: e.copy(out=Sbf[:, g, :], in_=S32[:, g, :]),
                      reads=[S32_b], writes=[Sbf_b])
                yield
                P.add("act", lambda e, gi=gi: e.activation(out=rs[gi][:], in_=rs[gi][:], func=AF.Exp, scale=-0.5),
                      reads=[rs_b[gi]], writes=[rs_b[gi]])
                yield
                P.add("dve", lambda e, g=g, gi=gi: e.scalar_tensor_tensor(
                    out=ystg[j][:, g * 512:(g + 1) * 512], in0=t1[gi][:], scalar=rs[gi][:, 0:1],
                    in1=nw[:, g * 512:(g + 1) * 512], op0=ALU.mult, op1=ALU.mult),
                    reads=[t1_b[gi], rs_b[gi], nw_b], writes=[ystg_b[j]])
                yield
            P.add("sp", lambda e: e.dma_start(out=self.YS[r0:r0 + 128, :], in_=ystg[j][:]),
                  reads=[ystg_b[j]], writes=[self.YS_b], dma_key=ystg_b[j].name)

        def all_tiles():
            load(0, 0)
            for tl in range(NTL):
                yield from tile(tl)

        return all_tiles()

    def merge_phase(self):
        nc, P = self.nc, self.P
        T, S = self.T, self.S
        NT = T // 512
        tag = "p5"
        inp = self.inp
        ps, psb = self.ps, self.psb
        with contextlib.ExitStack() as ctx:
            def wload(name, src, c0, ncol):
                w = self.sb(ctx, tag + name, [128, 8, ncol], BF16)
                w_b = P.buf(tag + name, multi=True)
                v = src.rearrange("(k p) f -> p k f", p=128)
                for k in range(8):
                    P.add("pool", lambda e, k=k: e.dma_start(out=w[:, k, :], in_=v[:, k, c0:c0 + ncol],
                                                              max_dma_last_dim=4096),
                          reads=[self.wbuf], writes=[w_b], dma_key=w_b.name)
                return w, w_b

            wbs, wbs_b = wload("wbs", inp["w_branch_ssd"], 0, 1024)
            wg, wg_b = wload("wg", inp["w_in"], 5664, 2048)
            wbf, wbf_b = wload("wbf", inp["w_branch_fox"], 0, 1024)
            wo, wo_b = wload("wo", inp["w_out"], 0, 1024)
            hTs = [self.sb(ctx, f"{tag}hT{i}", [128, 8, 512], BF16) for i in range(2)]
            hTs_b = [P.buf(f"{tag}hT{i}") for i in range(2)]
            yst4 = [self.sb(ctx, f"{tag}yst4{i}", [128, 4, 1024], BF16) for i in range(2)]
            yst4_b = [P.buf(f"{tag}yst4{i}") for i in range(2)]
            yfT = [self.sb(ctx, f"{tag}yfT{i}", [128, 8, 512], BF16) for i in range(2)]
            yfT_b = [P.buf(f"{tag}yfT{i}") for i in range(2)]
            ysT = self.sb(ctx, tag + "ysT", [128, 8, 512], BF16)
            ysT_b = P.buf(tag + "ysT")
            mT = self.sb(ctx, tag + "mT", [128, 8, 512], BF16)
            mT_b = P.buf(tag + "mT")
            sg = [self.sb(ctx, f"{tag}sg{i}", [128, 512], F32) for i in range(2)]
            sg_b = [P.buf(f"{tag}sg{i}") for i in range(2)]
            tm = [self.sb(ctx, f"{tag}tm{i}", [128, 512], F32) for i in range(2)]
            tm_b = [P.buf(f"{tag}tm{i}") for i in range(2)]
            xr = [self.sb(ctx, f"{tag}xr{i}", [128, 512], F32) for i in range(2)]
            xr_b = [P.buf(f"{tag}xr{i}") for i in range(2)]

            def loads(it, j):
                r0 = it * 512
                P.add("sp", lambda e: e.dma_start(
                    out=yst4[j][:], in_=self.YS[r0:r0 + 512, :].rearrange("(s p) d -> p s d", p=128)),
                    reads=[self.YS_b], writes=[yst4_b[j]], dma_key=yst4_b[j].name)
                P.add("sp", lambda e: e.dma_start(
                    out=yfT[j][:], in_=self.YFT[:, r0:r0 + 512].rearrange("(k p) t -> p k t", p=128)),
                    reads=[self.YFT_b], writes=[yfT_b[j]], dma_key=yfT_b[j].name)
                P.add("sp", lambda e: e.dma_start(
                    out=hTs[j][:], in_=self.HT[:, r0:r0 + 512].rearrange("(k p) t -> p k t", p=128)),
                    reads=[self.HT_b], writes=[hTs_b[j]], dma_key=hTs_b[j].name)

            def ys_transposes(j):
                self.fe_b({"hn4": yst4[j], "hn4_b": yst4_b[j], "hT": ysT, "hT_b": ysT_b}, (6, 7))

            n2 = [0]
            nr = [0]

            def body(it, j):
                r0 = it * 512
                hT, hb = hTs[j], hTs_b[j]
                for c in range(8):
                    for part in range(2):
                        jj = n2[0] % 2
                        n2[0] += 1
                        pA, pAb = ps[2 * jj], psb[2 * jj]
                        pG, pGb = ps[2 * jj + 1], psb[2 * jj + 1]
                        wbr, wbr_b = (wbs, wbs_b) if part == 0 else (wbf, wbf_b)
                        yT, yT_b = (ysT, ysT_b) if part == 0 else (yfT[j], yfT_b[j])
                        for k in range(8):
                            P.add("pe", lambda e, k=k, c=c, pA=pA, wbr=wbr, yT=yT: e.matmul(
                                out=pA[:], lhsT=wbr[:, k, c * 128:(c + 1) * 128], rhs=yT[:, k, :],
                                start=(k == 0), stop=(k == 7)), reads=[wbr_b, yT_b],
                                writes=[pAb, pGb] if k == 0 else [pAb])
                        gc = part * 1024 + c * 128
                        for k in range(8):
                            P.add("pe", lambda e, k=k, gc=gc, pG=pG: e.matmul(
                                out=pG[:], lhsT=wg[:, k, gc:gc + 128], rhs=hT[:, k, :],
                                start=(k == 0), stop=(k == 7)), reads=[wg_b, hb], writes=[pGb])
                        P.add("act", lambda e, pG=pG, part=part: e.activation(out=sg[part][:], in_=pG[:], func=AF.Sigmoid),
                              reads=[pGb], writes=[sg_b[part]])
                        P.add("dve", lambda e, pA=pA, part=part: e.tensor_tensor(
                            out=tm[part][:], in0=sg[part][:], in1=pA[:], op=ALU.mult),
                            reads=[sg_b[part], pAb], writes=[tm_b[part]])
                    P.add("dve", lambda e, c=c: e.tensor_tensor(out=mT[:, c, :], in0=tm[0][:], in1=tm[1][:], op=ALU.add),
                          reads=[tm_b[0], tm_b[1]], writes=[mT_b])
                for sub in range(4):
                    for half in range(2):
                        jr = nr[0] % 2
                        nr[0] += 1
                        pc, pcb = ps[4 + jr], psb[4 + jr]
                        rr = r0 + sub * 128
                        P.add("sp", lambda e, jr=jr, rr=rr, half=half: e.dma_start(
                            out=xr[jr][:], in_=self.X1[rr:rr + 128, half * 512:(half + 1) * 512]),
                            reads=[self.X1_b], writes=[xr_b[jr]], dma_key=xr_b[jr].name)
                        for k in range(8):
                            P.add("pe", lambda e, k=k, pc=pc, sub=sub, half=half: e.matmul(
                                out=pc[:], lhsT=mT[:, k, sub * 128:(sub + 1) * 128],
                                rhs=wo[:, k, half * 512:(half + 1) * 512],
                                start=(k == 0), stop=(k == 7)), reads=[wo_b, mT_b], writes=[pcb])
                        P.add("dve", lambda e, jr=jr, pc=pc: e.tensor_tensor(
                            out=xr[jr][:], in0=pc[:], in1=xr[jr][:], op=ALU.add),
                            reads=[pcb, xr_b[jr]], writes=[xr_b[jr]])
                        P.add("sp", lambda e, jr=jr, rr=rr, half=half: e.dma_start(
                            out=self.X2[rr:rr + 128, half * 512:(half + 1) * 512], in_=xr[jr][:]),
                            reads=[xr_b[jr]], writes=[self.X2_b], dma_key=xr_b[jr].name)

            loads(0, 0)
            ys_transposes(0)
            for it in range(NT):
                j = it % 2
                if it + 1 < NT:
                    loads(it + 1, (it + 1) % 2)
                body(it, j)
                if it + 1 < NT:
                    ys_transposes((it + 1) % 2)
            P.barrier()


_CACHE = {}


def _get_nc(S, stages="all", debug_outs=()):
    key = (S, stages, tuple(debug_outs))
    if key not in _CACHE:
        b = Builder(S, stages, debug_outs)
        nc = b.build()
        _CACHE[key] = (nc, b)
    return _CACHE[key]


def run(inputs, S, stages="all", debug_outs=(), trace=False):
    nc, b = _get_nc(S, stages, debug_outs)
    x = np.ascontiguousarray(inputs["x"], dtype=np.float32)
    B = x.shape[0]
    assert B == 2 * NCORES and x.shape[1] == S
    in_maps = []
    for c in range(NCORES):
        m = {"x": np.ascontiguousarray(x[2 * c:2 * c + 2].reshape(2 * S, D))}
        for n in b.inp:
            a = np.asarray(inputs[n], dtype=np.float32)
            m[n] = np.ascontiguousarray(a[0].reshape(tuple(b.inp[n].shape)))
        in_maps.append(m)
    res = run_bass_kernel_spmd(nc, in_maps, core_ids=list(range(NCORES)), trace=trace)
    out = np.concatenate([r["out"].reshape(2, S, D) for r in res.results], axis=0)
    return out, res


def kernel(**inputs):
    out, _ = run(inputs, 4096)
    return out.astype(np.float32)
```

```python
import contextlib
import numpy as np
import concourse.bass as bass
import concourse.mybir as mybir
from concourse.bass_utils import run_bass_kernel_spmd

F32 = mybir.dt.float32
BF16 = mybir.dt.bfloat16
AF = mybir.ActivationFunctionType
ALU = mybir.AluOpType
AX = mybir.AxisListType

D = 1024
DFF = 2816
NFF = DFF // 128
EPS = 1e-6
NCORES = 8


class Buf:
    __slots__ = ("name", "writers", "readers", "multi")

    def __init__(self, name, multi=False):
        self.name = name
        self.writers = {}
        self.readers = {}
        self.multi = multi


class Op:
    __slots__ = ("eng", "fn", "needs", "own", "is_dma")


class Prog:
    ENGS = ("pe", "act", "dve", "pool", "sp")

    def __init__(self):
        self.streams = {e: [] for e in self.ENGS}
        self.count = {e: 0 for e in self.ENGS}
        self.waited = {e: {} for e in self.ENGS}
        self.signaled = {e: set() for e in self.ENGS}
        self.dma_count = {}
        self.key2phys = {}
        self.free_phys = []
        self.nphys = 0
        self.live = []
        self.barrier_needs = {e: {} for e in self.ENGS}

    def buf(self, name, multi=False, track=True):
        b = Buf(name, multi)
        if track:
            self.live.append(b)
        return b

    def add(self, eng, fn, reads=(), writes=(), dma_key=None):
        op = Op()
        op.eng = eng
        op.fn = fn
        op.is_dma = dma_key is not None
        if op.is_dma:
            phys = self.key2phys.get(dma_key)
            if phys is None:
                if self.free_phys:
                    phys = self.free_phys.pop()
                else:
                    phys = self.nphys
                    self.nphys += 1
                self.key2phys[dma_key] = phys
            k = ("dma", phys)
            self.dma_count[k] = self.dma_count.get(k, 0) + 16
            own = (k, self.dma_count[k])
        else:
            self.count[eng] += 1
            own = (eng, self.count[eng])
        op.own = own
        needs = {}

        def need(k, v):
            if k == "pe" and eng == "pe" and not op.is_dma:
                return
            if needs.get(k, 0) < v:
                needs[k] = v

        for k, v in self.barrier_needs[eng].items():
            need(k, v)
        self.barrier_needs[eng] = {}
        for b in reads:
            for k, v in b.writers.items():
                need(k, v)
        for b in writes:
            if not b.multi:
                for k, v in b.writers.items():
                    need(k, v)
            for k, v in b.readers.items():
                need(k, v)
        w = self.waited[eng]
        final = []
        for k, v in needs.items():
            if w.get(k, 0) >= v:
                continue
            w[k] = v
            final.append((k, v))
            if not isinstance(k, tuple):
                self.signaled[k].add(v)
        op.needs = final
        for b in reads:
            if b.readers.get(own[0], 0) < own[1]:
                b.readers[own[0]] = own[1]
        for b in writes:
            if b.multi:
                if b.writers.get(own[0], 0) < own[1]:
                    b.writers[own[0]] = own[1]
            else:
                b.writers = {own[0]: own[1]}
                b.readers = {}
        self.streams[eng].append(op)
        return op

    def barrier(self, drop=True):
        u = {}
        for b in self.live:
            for d in (b.writers, b.readers):
                for k, v in d.items():
                    if u.get(k, 0) < v:
                        u[k] = v
        for e in self.ENGS:
            bn = self.barrier_needs[e]
            for k, v in u.items():
                if bn.get(k, 0) < v:
                    bn[k] = v
        if drop:
            self.live = [b for b in self.live if b.multi == "dram"]
            self.free_phys.extend(sorted(set(self.key2phys.values()), reverse=True))
            self.key2phys = {}

    def emit(self, nc, stack):
        rank = {}
        for e in self.ENGS:
            rank[e] = {v: i + 1 for i, v in enumerate(sorted(self.signaled[e]))}
        sems = {}

        def sem(k):
            if k not in sems:
                nm = "s_" + (k if isinstance(k, str) else "d_" + str(k[1]))
                sems[k] = stack.enter_context(nc.semaphore(nm))
            return sems[k]

        for e in self.ENGS:
            sem(e)
        for k in self.dma_count:
            sem(k)
        block = stack.enter_context(nc.Block())
        engobj = {"pe": "tensor", "act": "scalar", "dve": "vector", "pool": "gpsimd", "sp": "sync"}
        nsem = len(sems)
        stats = {}
        for e in self.ENGS:
            ops = self.streams[e]
            stats[e] = len(ops)

            def body(eng, ops=ops, e=e):
                for op in ops:
                    for k, v in op.needs:
                        if isinstance(k, tuple):
                            eng.wait_ge(sems[k], v)
                        else:
                            eng.wait_ge(sems[k], rank[k][v])
                    if op.fn is None:
                        continue
                    ins = op.fn(eng)
                    if op.is_dma:
                        ins.then_inc(sems[op.own[0]], 16)
                    elif op.own[1] in rank[e]:
                        ins.then_inc(sems[e], 1)

            getattr(block, engobj[e])(body)
        return nsem, stats


class Builder:
    def __init__(self, S, stages="all", debug_outs=()):
        self.S = S
        self.T = 2 * S
        self.stages = stages
        self.debug_outs = debug_outs
        self.nc = bass.Bass("TRN2", target_bir_lowering=False)
        self.P = Prog()
        self.stack = contextlib.ExitStack()

    def dram_in(self, name, shape, dt=F32):
        return self.nc.dram_tensor(name, list(shape), dt, kind="ExternalInput").ap()

    def dram_out(self, name, shape, dt=F32):
        return self.nc.dram_tensor(name, list(shape), dt, kind="ExternalOutput").ap()

    def dram_scratch(self, name, shape, dt):
        if name in self.debug_outs:
            t = self.nc.dram_tensor(name, list(shape), dt, kind="ExternalOutput").ap()
        else:
            t = self.nc.dram_tensor(name, list(shape), dt).ap()
        b = self.P.buf(name, multi="dram")
        return t, b

    def sb(self, ctx, name, shape, dt):
        return ctx.enter_context(self.nc.sbuf_tensor(name, list(shape), dt))

    def build(self):
        nc, P = self.nc, self.P
        T = self.T
        with self.stack:
            st = self.stack
            self.x = self.dram_in("x", [T, D])
            self.x_buf = P.buf("x", multi="dram")
            shapes = {"ffn1_norm": [1, D], "ffn1_w1": [D, DFF], "ffn1_w3": [D, DFF], "ffn1_w2": [DFF, D],
                      "ffn2_norm": [1, D], "ffn2_w1": [D, DFF], "ffn2_w3": [D, DFF], "ffn2_w2": [DFF, D],
                      "mix_norm": [1, D], "w_in": [D, 7712], "conv_w": [4, 1536], "conv_b": [1, 1536],
                      "dt_bias": [1, 16], "a_log": [1, 16], "d_skip": [1, 16], "ssd_norm_w": [1, D],
                      "fox_b_f": [1, 16], "q_norm_w": [1, 64], "k_norm_w": [1, 64],
                      "w_branch_ssd": [D, D], "w_branch_fox": [D, D], "w_out": [D, D]}
            self.inp = {n: self.dram_in(n, shapes[n]) for n in shapes}
            S = self.S
            self.X1, self.X1_b = self.dram_scratch("X1", [T, D], F32)
            self.X2, self.X2_b = self.dram_scratch("X2", [T, D], F32)
            self.ZS, self.ZS_b = self.dram_scratch("ZS", [T, 1024], BF16)
            self.XSB, self.XSB_b = self.dram_scratch("XSB", [T, 1280], BF16)
            self.BCT, self.BCT_b = self.dram_scratch("BCT", [4, 128, T], BF16)
            self.DT, self.DT_b = self.dram_scratch("DT", [T, 16], F32)
            self.FT, self.FT_b = self.dram_scratch("FT", [16, T], F32)
            self.QKT, self.QKT_b = self.dram_scratch("QKT", [16, 128, T], BF16)
            self.V, self.V_b = self.dram_scratch("V", [T, 1024], BF16)
            self.CUM, self.CUM_b = self.dram_scratch("CUM", [2, 6, 16, S], BF16)
            self.YFT, self.YFT_b = self.dram_scratch("YFT", [1024, T], BF16)
            self.YS, self.YS_b = self.dram_scratch("YS", [T, 1024], BF16)
            self.HT, self.HT_b = self.dram_scratch("HT", [1024, T], BF16)
            self.wbuf = P.buf("weights_dram", multi="dram")
            self.out = self.dram_out("out", [T, D])
            self.out_buf = P.buf("out", multi="dram")

            self.ps = []
            self.psb = []
            for i in range(8):
                self.ps.append(st.enter_context(nc.psum_tensor(f"ps{i}", [128, 512], F32)))
                self.psb.append(P.buf(f"ps{i}", track=False))

            self.ident = st.enter_context(nc.sbuf_tensor("ident", [128, 128], BF16))
            self.ident_b = P.buf("ident", track=False)
            identf = st.enter_context(nc.sbuf_tensor("identf", [128, 128], F32))
            identf_b = P.buf("identf", track=False)
            self.identf, self.identf_b = identf, identf_b
            P.add("pool", lambda e: e.memset(identf[:], 0.0), writes=[identf_b])
            P.add("pool", lambda e: e.affine_select(identf[:], identf[:], [[-1, 128]], ALU.not_equal, 1.0,
                                                    base=0, channel_multiplier=1),
                  reads=[identf_b], writes=[identf_b])
            P.add("dve", lambda e: e.tensor_copy(self.ident[:], identf[:]), reads=[identf_b], writes=[self.ident_b])

            self.eps_t = st.enter_context(nc.sbuf_tensor("eps_t", [128, 1], F32))
            self.eps_b = P.buf("eps_t", track=False)
            P.add("pool", lambda e: e.memset(self.eps_t[:], EPS), writes=[self.eps_b])

            inp = self.inp
            if self.stages == "ffn1":
                self.ffn_phase("f1", self.x, self.x_buf, self.out, self.out_buf,
                               inp["ffn1_norm"], inp["ffn1_w1"], inp["ffn1_w3"], inp["ffn1_w2"])
            else:
                nst = {"p2": 2, "p3": 3, "p4": 4, "p5": 5, "all": 6}[self.stages]
                last = self.out if nst < 6 else None
                self.ffn_phase("f1", self.x, self.x_buf, self.X1, self.X1_b,
                               inp["ffn1_norm"], inp["ffn1_w1"], inp["ffn1_w3"], inp["ffn1_w2"])
                self.proj_phase()
                if nst >= 3:
                    self.attn_phase(with_ssd=(nst >= 4))
                if nst >= 5:
                    self.merge_phase()
                if nst >= 6:
                    self.ffn_phase("f2", self.X2, self.X2_b, self.out, self.out_buf,
                                   inp["ffn2_norm"], inp["ffn2_w1"], inp["ffn2_w3"], inp["ffn2_w2"])
                else:
                    P.add("sp", lambda e: e.dma_start(out=self.out[0:128, :], in_=self.X1[0:128, :]),
                          reads=[self.X1_b], writes=[self.out_buf], dma_key="dbgout")
            P.add("sp", None, reads=[self.out_buf])
            nsem, stats = P.emit(nc, st)
            self.info = (nsem, stats)
        return nc

    def fe_alloc(self, ctx, tag):
        P = self.P
        d = {}
        d["xs4"] = self.sb(ctx, tag + "xs4", [128, 4, D], F32)
        d["xs4_b"] = P.buf(tag + "xs4")
        d["hn4"] = self.sb(ctx, tag + "hn4", [128, 4, D], BF16)
        d["hn4_b"] = P.buf(tag + "hn4")
        d["ss4"] = self.sb(ctx, tag + "ss4", [128, 4], F32)
        d["ss4_b"] = P.buf(tag + "ss4")
        d["ln4"] = self.sb(ctx, tag + "ln4", [128, 4], F32)
        d["ln4_b"] = P.buf(tag + "ln4")
        d["rs4"] = self.sb(ctx, tag + "rs4", [128, 4], F32)
        d["rs4_b"] = P.buf(tag + "rs4")
        d["wn"] = self.sb(ctx, tag + "wn", [128, D], F32)
        d["wn_b"] = P.buf(tag + "wn")
        d["hT"] = self.sb(ctx, tag + "hT", [128, 8, 512], BF16)
        d["hT_b"] = P.buf(tag + "hT")
        return d

    def fe_a(self, d, src, src_b, r0):
        self.fe_load(d, src, src_b, r0)
        self.fe_stats(d)

    def fe_load(self, d, src, src_b, r0):
        P = self.P
        xs4 = d["xs4"]
        P.add("sp", lambda e: e.dma_start(out=xs4[:], in_=src[r0:r0 + 512, :].rearrange("(s p) d -> p s d", p=128)),
              reads=[src_b], writes=[d["xs4_b"]], dma_key=d["xs4_b"].name)

    def fe_stats(self, d):
        P = self.P
        xs4, hn4, ss4, ln4, rs4, wn = d["xs4"], d["hn4"], d["ss4"], d["ln4"], d["rs4"], d["wn"]
        for sub in range(4):
            P.add("act", lambda e, sub=sub: e.activation(out=hn4[:, sub, :], in_=xs4[:, sub, :], func=AF.Square,
                                                         accum_out=ss4[:, sub:sub + 1]),
                  reads=[d["xs4_b"]], writes=[d["hn4_b"], d["ss4_b"]])
        P.add("act", lambda e: e.activation(out=ln4[:], in_=ss4[:], func=AF.Ln, bias=self.eps_t[:, 0:1], scale=1.0 / D),
              reads=[d["ss4_b"], self.eps_b], writes=[d["ln4_b"]])
        P.add("act", lambda e: e.activation(out=rs4[:], in_=ln4[:], func=AF.Exp, scale=-0.5),
              reads=[d["ln4_b"]], writes=[d["rs4_b"]])
        for sub in range(4):
            P.add("dve", lambda e, sub=sub: e.scalar_tensor_tensor(
                out=hn4[:, sub, :], in0=xs4[:, sub, :], scalar=rs4[:, sub:sub + 1], in1=wn[:],
                op0=ALU.mult, op1=ALU.mult),
                reads=[d["xs4_b"], d["rs4_b"], d["wn_b"]], writes=[d["hn4_b"]])

    def fe_b(self, d, banks):
        P = self.P
        hn4, hT = d["hn4"], d["hT"]
        for sub in range(4):
            bi = banks[sub % 2]
            pt = self.ps[bi][:].bitcast(BF16)
            ptb = self.psb[bi]
            for k in range(8):
                P.add("pe", lambda e, k=k, sub=sub, pt=pt: e.transpose(
                    out=pt[:, k * 128:(k + 1) * 128], in_=hn4[:, sub, k * 128:(k + 1) * 128],
                    identity=self.ident[:]),
                    reads=[d["hn4_b"], self.ident_b], writes=[ptb])
            P.add("act", lambda e, sub=sub, pt=pt: e.copy(
                out=hT[:, :, sub * 128:(sub + 1) * 128], in_=pt.rearrange("p (k t) -> p k t", k=8)),
                reads=[ptb], writes=[d["hT_b"]])

    def ffn_phase(self, tag, src, src_b, dst, dst_b, norm_w, w1, w3, w2):
        nc, P = self.nc, self.P
        T = self.T
        NT = T // 512
        with contextlib.ExitStack() as ctx:
            w1b = self.sb(ctx, tag + "w1b", [128, 8, DFF], BF16)
            w3b = self.sb(ctx, tag + "w3b", [128, 8, DFF], BF16)
            w2b = self.sb(ctx, tag + "w2b", [128, NFF, D], BF16)
            WG = ((0, 4), (4, 10), (10, 16), (16, 22))
            w1_bs = [P.buf(f"{tag}w1b{g}", multi=True) for g in range(4)]
            w3_bs = [P.buf(f"{tag}w3b{g}", multi=True) for g in range(4)]
            gof = {c: gi_ for gi_, (a, b) in enumerate(WG) for c in range(a, b)}
            w2_b = P.buf(tag + "w2b", multi=True)
            fe = self.fe_alloc(ctx, tag)
            wn, wn_b = fe["wn"], fe["wn_b"]
            hT, hb = fe["hT"], fe["hT_b"]
            xr = [self.sb(ctx, f"{tag}xr{i}", [128, 512], F32) for i in range(2)]
            xr_b = [P.buf(f"{tag}xr{i}") for i in range(2)]
            g = self.sb(ctx, tag + "gT", [128, NFF, 512], BF16)
            gb = P.buf(tag + "gT")
            sl = [self.sb(ctx, f"{tag}sl{i}", [128, 512], F32) for i in range(2)]
            sl_b = [P.buf(f"{tag}sl{i}") for i in range(2)]

            w1v = w1.rearrange("(k p) f -> p k f", p=128)
            w3v = w3.rearrange("(k p) f -> p k f", p=128)
            w2v = w2.rearrange("(k p) f -> p k f", p=128)
            P.add("sp", lambda e: e.dma_start(out=wn[:], in_=norm_w.partition_broadcast(128)),
                  reads=[self.wbuf], writes=[wn_b], dma_key=wn_b.name)
            for wg_, (ca, cb_) in enumerate(WG):
                for k in range(8):
                    P.add("pool", lambda e, k=k, ca=ca, cb_=cb_: e.dma_start(
                        out=w1b[:, k, ca * 128:cb_ * 128], in_=w1v[:, k, ca * 128:cb_ * 128], max_dma_last_dim=4096),
                        reads=[self.wbuf], writes=[w1_bs[wg_]], dma_key=w1_bs[wg_].name)
                    P.add("pool", lambda e, k=k, ca=ca, cb_=cb_: e.dma_start(
                        out=w3b[:, k, ca * 128:cb_ * 128], in_=w3v[:, k, ca * 128:cb_ * 128], max_dma_last_dim=4096),
                        reads=[self.wbuf], writes=[w3_bs[wg_]], dma_key=w3_bs[wg_].name)
            for k in range(NFF):
                P.add("pool", lambda e, k=k: e.dma_start(out=w2b[:, k, :], in_=w2v[:, k, :], max_dma_last_dim=4096),
                      reads=[self.wbuf], writes=[w2_b], dma_key=w2_b.name)

            ps, psb = self.ps, self.psb
            nc_ = 0
            nr = 0
            self.fe_a(fe, src, src_b, 0)
            self.fe_b(fe, (6, 7))
            for it in range(NT):
                r0 = it * 512
                if it + 1 < NT:
                    self.fe_a(fe, src, src_b, r0 + 512)
                for c in range(NFF):
                    j = nc_ % 2
                    jp = ((0, 1), (2, 3), (6, 7))[nc_ % 3]
                    nc_ += 1
                    pa, pab = ps[jp[0]], psb[jp[0]]
                    pb, pbb = ps[jp[1]], psb[jp[1]]
                    for k in range(8):
                        P.add("pe", lambda e, k=k, c=c, pa=pa: e.matmul(
                            out=pa[:], lhsT=w1b[:, k, c * 128:(c + 1) * 128], rhs=hT[:, k, :],
                            start=(k == 0), stop=(k == 7)), reads=[w1_bs[gof[c]], hb], writes=[pab])
                    for k in range(8):
                        P.add("pe", lambda e, k=k, c=c, pb=pb: e.matmul(
                            out=pb[:], lhsT=w3b[:, k, c * 128:(c + 1) * 128], rhs=hT[:, k, :],
                            start=(k == 0), stop=(k == 7)), reads=[w3_bs[gof[c]], hb], writes=[pbb])
                    P.add("act", lambda e, pa=pa, j=j: e.activation(out=sl[j][:], in_=pa[:], func=AF.Silu),
                          reads=[pab], writes=[sl_b[j]])
                    P.add("dve", lambda e, pb=pb, j=j, c=c: e.tensor_tensor(
                        out=g[:, c, :], in0=sl[j][:], in1=pb[:], op=ALU.mult),
                        reads=[sl_b[j], pbb], writes=[gb])
                if it + 1 < NT:
                    self.fe_b(fe, (6, 7))
                for sub in range(4):
                    for half in range(2):
                        j = nr % 2
                        nr += 1
                        pc, pcb = ps[4 + j], psb[4 + j]
                        rr = r0 + sub * 128
                        P.add("sp", lambda e, j=j, rr=rr, half=half: e.dma_start(
                            out=xr[j][:], in_=src[rr:rr + 128, half * 512:(half + 1) * 512]),
                            reads=[src_b], writes=[xr_b[j]], dma_key=xr_b[j].name)
                        for c in range(NFF):
                            P.add("pe", lambda e, c=c, pc=pc, sub=sub, half=half: e.matmul(
                                out=pc[:], lhsT=g[:, c, sub * 128:(sub + 1) * 128],
                                rhs=w2b[:, c, half * 512:(half + 1) * 512],
                                start=(c == 0), stop=(c == NFF - 1)), reads=[w2_b, gb], writes=[pcb])
                        P.add("dve", lambda e, j=j, pc=pc: e.scalar_tensor_tensor(
                            out=xr[j][:], in0=pc[:], scalar=0.5, in1=xr[j][:], op0=ALU.mult, op1=ALU.add),
                            reads=[pcb, xr_b[j]], writes=[xr_b[j]])
                        P.add("sp", lambda e, j=j, rr=rr, half=half: e.dma_start(
                            out=dst[rr:rr + 128, half * 512:(half + 1) * 512], in_=xr[j][:]),
                            reads=[xr_b[j]], writes=[dst_b], dma_key=xr_b[j].name)
            P.barrier()


    def proj_phase(self):
        nc, P = self.nc, self.P
        T, S = self.T, self.S
        NT = T // 512
        NW = 5664
        tag = "p2"
        inp = self.inp
        ps, psb = self.ps, self.psb
        with contextlib.ExitStack() as ctx:
            wb = self.sb(ctx, tag + "win", [128, 8, NW], BF16)
            WR = ((1024, 2560), (0, 1024), (4624, 5664), (2560, 4624))
            wb_bs = [P.buf(f"{tag}win{g}", multi=True) for g in range(4)]

            def wbof(col):
                for g, (a, b) in enumerate(WR):
                    if a <= col < b:
                        return wb_bs[g]
                raise AssertionError(col)
            wv = inp["w_in"].rearrange("(k p) f -> p k f", p=128)
            fe = self.fe_alloc(ctx, tag)
            hT, hb = fe["hT"], fe["hT_b"]
            P.add("sp", lambda e: e.dma_start(out=fe["wn"][:], in_=inp["mix_norm"].partition_broadcast(128)),
                  reads=[self.wbuf], writes=[fe["wn_b"]], dma_key=fe["wn_b"].name)
            for g, (ca, cb_) in enumerate(WR):
                for k in range(8):
                    P.add("pool", lambda e, k=k, ca=ca, cb_=cb_: e.dma_start(
                        out=wb[:, k, ca:cb_], in_=wv[:, k, ca:cb_], max_dma_last_dim=4096),
                        reads=[self.wbuf], writes=[wb_bs[g]], dma_key=wb_bs[g].name)
            cw = self.sb(ctx, tag + "cw", [128, 12, 4], F32)
            cw_b = P.buf(tag + "cw", multi=True)
            cbias = self.sb(ctx, tag + "cbias", [128, 12], F32)
            cbias_b = P.buf(tag + "cbias", multi=True)
            for c in range(12):
                for j in range(4):
                    P.add("sp", lambda e, c=c, j=j: e.dma_start(
                        out=cw[:, c, j:j + 1], in_=inp["conv_w"][j:j + 1, c * 128:(c + 1) * 128].rearrange("o p -> p o")),
                        reads=[self.wbuf], writes=[cw_b], dma_key=cw_b.name)
                P.add("sp", lambda e, c=c: e.dma_start(
                    out=cbias[:, c:c + 1], in_=inp["conv_b"][0:1, c * 128:(c + 1) * 128].rearrange("o p -> p o")),
                    reads=[self.wbuf], writes=[cbias_b], dma_key=cbias_b.name)
            cbfull = self.sb(ctx, tag + "cbfull", [128, 1536], BF16)
            cbfull_b = P.buf(tag + "cbfull")
            P.add("pool", lambda e: e.memset(cbfull[:], 0.0), writes=[cbfull_b])
            P.add("pool", lambda e: e.dma_start(out=cbfull[0:1, :], in_=inp["conv_b"][0:1, :]),
                  reads=[self.wbuf, cbfull_b], writes=[cbfull_b], dma_key=cbfull_b.name)
            onesK = self.sb(ctx, tag + "onesK", [128, 128], BF16)
            onesK_b = P.buf(tag + "onesK")
            P.add("pool", lambda e: e.memset(onesK[:], 0.0), writes=[onesK_b])
            P.add("pool", lambda e: e.memset(onesK[0:1, :], 1.0), reads=[onesK_b], writes=[onesK_b])
            diagw = self.sb(ctx, tag + "diagw", [128, 12, 4, 128], BF16)
            diagw_b = P.buf(tag + "diagw")
            for c in range(12):
                for j in range(4):
                    P.add("dve", lambda e, c=c, j=j: e.tensor_scalar(
                        out=diagw[:, c, j, :], in0=self.identf[:], scalar1=cw[:, c, j:j + 1], scalar2=None,
                        op0=ALU.mult), reads=[self.identf_b, cw_b], writes=[diagw_b])
            wqk = self.sb(ctx, tag + "wqk", [128, 2], F32)
            wqk_b = P.buf(tag + "wqk", multi=True)
            for half in range(2):
                P.add("sp", lambda e, half=half: e.dma_start(
                    out=wqk[half * 64:(half + 1) * 64, 0:1], in_=inp["q_norm_w"].rearrange("o d -> d o")),
                    reads=[self.wbuf], writes=[wqk_b], dma_key=wqk_b.name)
                P.add("sp", lambda e, half=half: e.dma_start(
                    out=wqk[half * 64:(half + 1) * 64, 1:2], in_=inp["k_norm_w"].rearrange("o d -> d o")),
                    reads=[self.wbuf], writes=[wqk_b], dma_key=wqk_b.name)
            wqs = self.sb(ctx, tag + "wqs", [128, 2], F32)
            wqs_b = P.buf(tag + "wqs")
            P.add("dve", lambda e: e.tensor_copy(out=wqs[:], in_=wqk[:]), reads=[wqk_b], writes=[wqs_b])
            P.add("dve", lambda e: e.tensor_scalar(out=wqs[:, 0:1], in0=wqk[:, 0:1], scalar1=0.125, scalar2=None,
                                                   op0=ALU.mult), reads=[wqk_b, wqs_b], writes=[wqs_b])
            bd = self.sb(ctx, tag + "bd", [128, 128], BF16)
            bd_b = P.buf(tag + "bd")
            P.add("pool", lambda e: e.memset(bd[:], 0.0), writes=[bd_b])
            P.add("pool", lambda e: e.memset(bd[0:64, 0:64], 1.0), reads=[bd_b], writes=[bd_b])
            P.add("pool", lambda e: e.memset(bd[64:128, 64:128], 1.0), reads=[bd_b], writes=[bd_b])

            xbc = self.sb(ctx, tag + "xbc", [128, 12, 515], BF16)
            xbc_b = P.buf(tag + "xbc")
            csb = self.sb(ctx, tag + "csb", [128, 10, 512], BF16)
            csb_b = P.buf(tag + "csb")
            zst = [self.sb(ctx, f"{tag}zst{i}", [128, 1024], BF16) for i in range(2)]
            zst_b = [P.buf(f"{tag}zst{i}") for i in range(2)]
            xst = [self.sb(ctx, f"{tag}xst{i}", [128, 1280], BF16) for i in range(2)]
            xst_b = [P.buf(f"{tag}xst{i}") for i in range(2)]
            vst = [self.sb(ctx, f"{tag}vst{i}", [128, 1024], BF16) for i in range(2)]
            vst_b = [P.buf(f"{tag}vst{i}") for i in range(2)]
            dtst = self.sb(ctx, tag + "dtst", [128, 4, 16], F32)
            dtst_b = P.buf(tag + "dtst")
            fst = self.sb(ctx, tag + "fst", [16, 512], F32)
            fst_b = P.buf(tag + "fst")
            NR = 4
            fm = [self.sb(ctx, f"{tag}fm{i}", [128, 512], BF16) for i in range(NR)]
            fm_b = [P.buf(f"{tag}fm{i}") for i in range(NR)]
            sq = [self.sb(ctx, f"{tag}sq{i}", [128, 512], BF16) for i in range(2)]
            sq_b = [P.buf(f"{tag}sq{i}") for i in range(2)]
            lr = [self.sb(ctx, f"{tag}lr{i}", [128, 512], F32) for i in range(2)]
            lr_b = [P.buf(f"{tag}lr{i}") for i in range(2)]

            rot = [0]

            def nb():
                i = rot[0] % 6
                rot[0] += 1
                return ps[i], psb[i]

            cnt = {"fm": 0, "z": 0, "x": 0, "v": 0, "qk": 0}

            def proj_fm(col0, M):
                pa, pab = nb()
                for k in range(8):
                    P.add("pe", lambda e, k=k, pa=pa: e.matmul(
                        out=pa[0:M, :], lhsT=wb[:, k, col0:col0 + M], rhs=hT[:, k, :],
                        start=(k == 0), stop=(k == 7)), reads=[wbof(col0), hb], writes=[pab])
                return pa, pab

            def proj_tm(col0, N, sub, pa, pab, o0=0):
                for k in range(8):
                    P.add("pe", lambda e, k=k: e.matmul(
                        out=pa[:, o0:o0 + N], lhsT=hT[:, k, sub * 128:(sub + 1) * 128], rhs=wb[:, k, col0:col0 + N],
                        start=(k == 0), stop=(k == 7)), reads=[wbof(col0), hb], writes=[pab])

            self.fe_a(fe, self.X1, self.X1_b, 0)
            self.fe_b(fe, (6, 7))
            for it in range(NT):
                r0 = it * 512
                P.add("sp", lambda e, r0=r0: e.dma_start(
                    out=self.HT[:, r0:r0 + 512].rearrange("(k p) t -> p k t", p=128), in_=hT[:]),
                    reads=[hb], writes=[self.HT_b], dma_key=tag + "hTst")
                if it + 1 < NT:
                    self.fe_load(fe, self.X1, self.X1_b, r0 + 512)
                if r0 % S == 0:
                    P.add("pool", lambda e: e.memset(xbc[:, :, 0:3], 0.0), writes=[xbc_b])
                for c in range(12):
                    pa, pab = proj_fm(1024 + c * 128, 128)
                    P.add("act", lambda e, c=c, pa=pa: e.copy(out=xbc[:, c, 3:515], in_=pa[:]),
                          reads=[pab], writes=[xbc_b])
                for c in range(12):
                    pa, pab = nb()
                    for tap in range(4):
                        P.add("pe", lambda e, c=c, tap=tap, pa=pa: e.matmul(
                            out=pa[:], lhsT=diagw[:, c, tap, :], rhs=xbc[:, c, tap:tap + 512],
                            start=(tap == 0), stop=(tap == 3)), reads=[xbc_b, diagw_b], writes=[pab])
                    if c < 10:
                        P.add("act", lambda e, c=c, pa=pa: e.activation(
                            out=csb[:, c, :], in_=pa[:], func=AF.Silu, bias=cbias[:, c:c + 1]),
                            reads=[pab, cbias_b], writes=[csb_b])
                        if c >= 8:
                            P.add("sp", lambda e, c=c, r0=r0: e.dma_start(
                                out=self.BCT[c - 8, :, r0:r0 + 512], in_=csb[:, c, :]),
                                reads=[csb_b], writes=[self.BCT_b], dma_key=f"{tag}csb{c}")
                    else:
                        j = cnt["fm"] % NR
                        cnt["fm"] += 1
                        P.add("act", lambda e, c=c, pa=pa, j=j: e.activation(
                            out=fm[j][:], in_=pa[:], func=AF.Silu, bias=cbias[:, c:c + 1]),
                            reads=[pab, cbias_b], writes=[fm_b[j]])
                        P.add("sp", lambda e, c=c, j=j, r0=r0: e.dma_start(
                            out=self.BCT[c - 8, :, r0:r0 + 512], in_=fm[j][:]),
                            reads=[fm_b[j]], writes=[self.BCT_b], dma_key=fm_b[j].name)
                for sub in range(4):
                    j = cnt["x"] % 2
                    cnt["x"] += 1
                    pA, pAb = nb()
                    pB, pBb = nb()
                    pAv = pA[:].bitcast(BF16)
                    pBv = pB[:].bitcast(BF16)
                    for c in range(10):
                        dstv = pAv[:, c * 128:(c + 1) * 128] if c < 8 else pBv[:, (c - 8) * 128:(c - 7) * 128]
                        P.add("pe", lambda e, c=c, sub=sub, dstv=dstv: e.transpose(
                            out=dstv, in_=csb[:, c, sub * 128:(sub + 1) * 128], identity=self.ident[:]),
                            reads=[csb_b, self.ident_b], writes=[pAb if c < 8 else pBb])
                    P.add("act", lambda e, j=j, pAv=pAv: e.copy(out=xst[j][:, 0:1024], in_=pAv),
                          reads=[pAb], writes=[xst_b[j]])
                    P.add("dve", lambda e, j=j, pBv=pBv: e.tensor_copy(out=xst[j][:, 1024:1280], in_=pBv[:, 0:256]),
                          reads=[pBb], writes=[xst_b[j]])
                    rr = r0 + sub * 128
                    P.add("sp", lambda e, j=j, rr=rr: e.dma_start(out=self.XSB[rr:rr + 128, :], in_=xst[j][:]),
                          reads=[xst_b[j]], writes=[self.XSB_b], dma_key=xst_b[j].name)
                P.add("pool", lambda e: e.tensor_copy(out=xbc[:, :, 0:3], in_=xbc[:, :, 512:515]),
                      reads=[xbc_b], writes=[xbc_b])
                for sub in range(4):
                    rr = r0 + sub * 128
                    j = cnt["z"] % 2
                    cnt["z"] += 1
                    for half in range(2):
                        pa, pab = nb()
                        proj_tm(half * 512, 512, sub, pa, pab)
                        P.add("act", lambda e, pa=pa, j=j, half=half: e.activation(
                            out=zst[j][:, half * 512:(half + 1) * 512], in_=pa[:], func=AF.Silu),
                            reads=[pab], writes=[zst_b[j]])
                    P.add("sp", lambda e, j=j, rr=rr: e.dma_start(out=self.ZS[rr:rr + 128, :], in_=zst[j][:]),
                          reads=[zst_b[j]], writes=[self.ZS_b], dma_key=zst_b[j].name)
                    for half in range(2):
                        pa, pab = nb()
                        proj_tm(4624 + half * 512, 512, sub, pa, pab)
                        P.add("dve", lambda e, pa=pa, j=j, half=half: e.tensor_copy(
                            out=vst[j][:, half * 512:(half + 1) * 512], in_=pa[:]),
                            reads=[pab], writes=[vst_b[j]])
                    P.add("sp", lambda e, j=j, rr=rr: e.dma_start(out=self.V[rr:rr + 128, :], in_=vst[j][:]),
                          reads=[vst_b[j]], writes=[self.V_b], dma_key=vst_b[j].name)
                pa, pab = nb()
                for sub in range(4):
                    proj_tm(2560, 16, sub, pa, pab, o0=sub * 16)
                P.add("dve", lambda e, pa=pa: e.tensor_copy(
                    out=dtst[:], in_=pa[:, 0:64].rearrange("p (s h) -> p s h", s=4)),
                    reads=[pab], writes=[dtst_b])
                P.add("sp", lambda e, r0=r0: e.dma_start(
                    out=self.DT[r0:r0 + 512, :].rearrange("(s p) h -> p s h", p=128), in_=dtst[:]),
                    reads=[dtst_b], writes=[self.DT_b], dma_key=dtst_b.name)
                pa, pab = proj_fm(5648, 16)
                P.add("dve", lambda e, pa=pa: e.tensor_copy(out=fst[:], in_=pa[0:16, :]),
                      reads=[pab], writes=[fst_b])
                P.add("sp", lambda e, r0=r0: e.dma_start(out=self.FT[:, r0:r0 + 512], in_=fst[:]),
                      reads=[fst_b], writes=[self.FT_b], dma_key=fst_b.name)
                def qk_tail(ch, pa, pab, jq, r0=r0):
                    pb, pbb = nb()
                    P.add("pe", lambda e: e.matmul(out=pb[:], lhsT=bd[:], rhs=sq[jq][:], start=True, stop=True),
                          reads=[bd_b, sq_b[jq]], writes=[pbb])
                    P.add("act", lambda e: e.activation(
                        out=lr[jq][:], in_=pb[:], func=AF.Ln, bias=self.eps_t[:, 0:1], scale=1.0 / 64),
                        reads=[pbb, self.eps_b], writes=[lr_b[jq]])
                    P.add("act", lambda e: e.activation(out=lr[jq][:], in_=lr[jq][:], func=AF.Exp, scale=-0.5),
                          reads=[lr_b[jq]], writes=[lr_b[jq]])
                    j = cnt["fm"] % NR
                    cnt["fm"] += 1
                    wi = 0 if ch < 8 else 1
                    P.add("dve", lambda e: e.scalar_tensor_tensor(
                        out=fm[j][:], in0=pa[:], scalar=wqs[:, wi:wi + 1], in1=lr[jq][:],
                        op0=ALU.mult, op1=ALU.mult), reads=[pab, wqs_b, lr_b[jq]], writes=[fm_b[j]])
                    P.add("sp", lambda e: e.dma_start(out=self.QKT[ch, :, r0:r0 + 512], in_=fm[j][:]),
                          reads=[fm_b[j]], writes=[self.QKT_b], dma_key=fm_b[j].name)

                prev = None
                for ch in range(16):
                    if ch == 8 and it + 1 < NT:
                        self.fe_stats(fe)
                    pa, pab = proj_fm(2576 + ch * 128, 128)
                    jq = cnt["qk"] % 2
                    cnt["qk"] += 1
                    P.add("act", lambda e, pa=pa, jq=jq: e.activation(out=sq[jq][:], in_=pa[:], func=AF.Square),
                          reads=[pab], writes=[sq_b[jq]])
                    if prev is not None:
                        qk_tail(*prev)
                    prev = (ch, pa, pab, jq)
                qk_tail(*prev)
                if it + 1 < NT:
                    self.fe_b(fe, (6, 7))
            P.barrier()


    def attn_phase(self, with_ssd=False):
        nc, P = self.nc, self.P
        T, S = self.T, self.S
        NG = S // 512
        NKB = S // 128
        tag = "p3"
        inp = self.inp
        ps, psb = self.ps, self.psb
        with contextlib.ExitStack() as ctx:
            mk = self.sb(ctx, tag + "mk", [128, 128], BF16)
            mk_b = P.buf(tag + "mk")
            P.add("pool", lambda e: e.memset(mk[:], 0.0), writes=[mk_b])
            P.add("pool", lambda e: e.affine_select(mk[:], mk[:], [[1, 128]], ALU.is_ge, -30000.0, base=0,
                                                    channel_multiplier=-1), reads=[mk_b], writes=[mk_b])
            Sh = self.sb(ctx, tag + "Sh", [128, 128], F32)
            Sh_b = P.buf(tag + "Sh")
            P.add("pool", lambda e: e.memset(Sh[:], 0.0), writes=[Sh_b])
            P.add("pool", lambda e: e.affine_select(Sh[:], Sh[:], [[-1, 128]], ALU.not_equal, 1.0, base=-64,
                                                    channel_multiplier=1), reads=[Sh_b], writes=[Sh_b])
            R = self.sb(ctx, tag + "R", [128, 512], F32)
            R_b = P.buf(tag + "R")
            P.add("pool", lambda e: e.memset(R[:], 0.0), writes=[R_b])
            bfb = self.sb(ctx, tag + "bfb", [16, 1], F32)
            bfb_b = P.buf(tag + "bfb")
            P.add("sp", lambda e: e.dma_start(out=bfb[:], in_=inp["fox_b_f"].rearrange("o h -> h o")),
                  reads=[self.wbuf], writes=[bfb_b], dma_key=bfb_b.name)
            nbf = self.sb(ctx, tag + "nbf", [16, 1], F32)
            nbf_b = P.buf(tag + "nbf")
            P.add("dve", lambda e: e.tensor_scalar(out=nbf[:], in0=bfb[:], scalar1=-1.0, scalar2=None, op0=ALU.mult),
                  reads=[bfb_b], writes=[nbf_b])
            qp = [self.sb(ctx, f"{tag}qp{i}", [128, S], BF16) for i in range(2)]
            qp_b = [P.buf(f"{tag}qp{i}", multi=True) for i in range(2)]
            kp = [self.sb(ctx, f"{tag}kp{i}", [128, S], BF16) for i in range(2)]
            kp_b = [P.buf(f"{tag}kp{i}", multi=True) for i in range(2)]
            vp = [self.sb(ctx, f"{tag}vp{i}", [128, NKB, 128], BF16) for i in range(2)]
            vp_b = [P.buf(f"{tag}vp{i}", multi=True) for i in range(2)]
            for i in range(2):
                P.add("pool", lambda e, i=i: e.memset(qp[i][64:70, :], 1.0), writes=[qp_b[i]])
                P.add("pool", lambda e, i=i: e.memset(kp[i][64:70, :], 1.0), writes=[kp_b[i]])
                P.add("pool", lambda e, i=i: e.memset(vp[i][:], 1.0), writes=[vp_b[i]])
            pt = [self.sb(ctx, f"{tag}pt{i}", [128, 512], BF16) for i in range(4)]
            pt_b = [P.buf(f"{tag}pt{i}") for i in range(4)]
            osb = [self.sb(ctx, f"{tag}osb{i}", [64, 512], F32) for i in range(2)]
            osb_b = [P.buf(f"{tag}osb{i}") for i in range(2)]
            yst = [self.sb(ctx, f"{tag}yst{i}", [64, 512], BF16) for i in range(2)]
            yst_b = [P.buf(f"{tag}yst{i}") for i in range(2)]

            with contextlib.ExitStack() as c2:
                fsb = self.sb(c2, tag + "fsb", [16, S], F32)
                fsb_b = P.buf(tag + "fsb")
                cum = self.sb(c2, tag + "cum", [16, S], F32)
                cum_b = P.buf(tag + "cum")
                rr_ = self.sb(c2, tag + "rr", [16, S], F32)
                rr_b = P.buf(tag + "rr")
                one16 = self.sb(c2, tag + "one16", [16, S], BF16)
                one16_b = P.buf(tag + "one16")
                cp = self.sb(c2, tag + "cp", [16, 6, S], BF16)
                cp_b = P.buf(tag + "cp")
                P.add("pool", lambda e: e.memset(one16[:], 1.0), writes=[one16_b])
                for seq in range(2):
                    P.add("sp", lambda e, seq=seq: e.dma_start(out=fsb[:], in_=self.FT[:, seq * S:(seq + 1) * S]),
                          reads=[self.FT_b], writes=[fsb_b], dma_key=fsb_b.name)
                    P.add("act", lambda e: e.activation(out=fsb[:], in_=fsb[:], func=AF.Exp, bias=nbf[:, 0:1], scale=-1.0),
                          reads=[fsb_b, nbf_b], writes=[fsb_b])
                    P.add("act", lambda e: e.activation(out=fsb[:], in_=fsb[:], func=AF.Ln, bias=1.0),
                          reads=[fsb_b], writes=[fsb_b])
                    P.add("dve", lambda e: e.tensor_tensor_scan(out=cum[:], data0=one16[:], data1=fsb[:], initial=0.0,
                                                                op0=ALU.mult, op1=ALU.subtract),
                          reads=[one16_b, fsb_b], writes=[cum_b])
                    P.add("dve", lambda e: e.tensor_copy(out=cp[:, 0, :], in_=cum[:]), reads=[cum_b], writes=[cp_b])
                    P.add("dve", lambda e: e.tensor_tensor(out=rr_[:], in0=cum[:], in1=cp[:, 0, :], op=ALU.subtract),
                          reads=[cum_b, cp_b], writes=[rr_b])
                    P.add("dve", lambda e: e.tensor_copy(out=cp[:, 1, :], in_=rr_[:]), reads=[rr_b, cp_b], writes=[cp_b])
                    P.add("dve", lambda e: e.tensor_tensor(out=cum[:], in0=rr_[:], in1=cp[:, 1, :], op=ALU.subtract),
                          reads=[rr_b, cp_b], writes=[cum_b])
                    P.add("dve", lambda e: e.tensor_copy(out=cp[:, 2, :], in_=cum[:]), reads=[cum_b, cp_b], writes=[cp_b])
                    P.add("dve", lambda e: e.tensor_scalar(out=cp[:, 3:6, :], in0=cp[:, 0:3, :], scalar1=-1.0, scalar2=None,
                                                           op0=ALU.mult), reads=[cp_b], writes=[cp_b])
                    P.add("sp", lambda e, seq=seq: e.dma_start(out=self.CUM[seq].rearrange("j h s -> h j s"), in_=cp[:]),
                          reads=[cp_b], writes=[self.CUM_b], dma_key=cp_b.name)
                P.barrier(drop=False)

            def load(seq, h, j):
                c0 = seq * S
                hp, ho = h // 2, (h % 2) * 64
                P.add("sp", lambda e: e.dma_start(out=qp[j][0:64, :], in_=self.QKT[hp, ho:ho + 64, c0:c0 + S]),
                      reads=[self.QKT_b], writes=[qp_b[j]], dma_key=qp_b[j].name)
                P.add("sp", lambda e: e.dma_start(out=qp[j][64:67, :], in_=self.CUM[seq, 0:3, h, :]),
                      reads=[self.CUM_b], writes=[qp_b[j]], dma_key=qp_b[j].name)
                P.add("sp", lambda e: e.dma_start(out=kp[j][0:64, :], in_=self.QKT[8 + hp, ho:ho + 64, c0:c0 + S]),
                      reads=[self.QKT_b], writes=[kp_b[j]], dma_key=kp_b[j].name)
                P.add("sp", lambda e: e.dma_start(out=kp[j][67:70, :], in_=self.CUM[seq, 3:6, h, :]),
                      reads=[self.CUM_b], writes=[kp_b[j]], dma_key=kp_b[j].name)
                P.add("sp", lambda e: e.dma_start(
                    out=vp[j][:, :, 0:64],
                    in_=self.V[c0:c0 + S, h * 64:(h + 1) * 64].rearrange("(kb p) d -> p kb d", p=128)),
                    reads=[self.V_b], writes=[vp_b[j]], dma_key=vp_b[j].name)

            heads = [(seq, h) for seq in range(2) for h in range(16)]
            steps = []
            for hi, (seq, h) in enumerate(heads):
                for G in range(NG):
                    nkb = 4 * (G + 1)
                    for kb in range(nkb):
                        steps.append((hi, seq, h, G, kb, nkb))
            LA = 3
            LC = 6
            Rr = [R, self.sb(ctx, tag + "R1", [128, 512], F32)]
            Rr_b = [R_b, P.buf(tag + "R1")]
            P.add("pool", lambda e: e.memset(Rr[1][:], 0.0), writes=[Rr_b[1]])

            def stageA(i):
                hi, seq, h, G, kb, nkb = steps[i]
                j = hi % 2
                if G == 0 and kb == 3 and hi + 1 < len(heads):
                    load(heads[hi + 1][0], heads[hi + 1][1], (hi + 1) % 2)
                d = kb - 4 * G
                c0 = d * 128 if d > 0 else 0
                r = i % 4
                pS, pSb = ps[i % 3], psb[i % 3]
                P.add("pe", lambda e: e.matmul(
                    out=pS[:, c0:512], lhsT=kp[j][0:70, kb * 128:(kb + 1) * 128],
                    rhs=qp[j][0:70, G * 512 + c0:(G + 1) * 512], start=True, stop=(d < 0)),
                    reads=[kp_b[j], qp_b[j]] + ([pt_b[(i - LA) % 4]] if i - LA >= 0 else []), writes=[pSb])
                if d >= 0:
                    P.add("pe", lambda e: e.matmul(
                        out=pS[:, d * 128:(d + 1) * 128], lhsT=self.ident[:], rhs=mk[:],
                        start=False, stop=True), reads=[self.ident_b, mk_b], writes=[pSb])
                P.add("act", lambda e: e.activation(out=pt[r][:, c0:512], in_=pS[:, c0:512], func=AF.Exp),
                      reads=[pSb], writes=[pt_b[r]])

            gcount = [0]
            pending = []

            def stageB(i):
                hi, seq, h, G, kb, nkb = steps[i]
                j = hi % 2
                d = kb - 4 * G
                c0 = d * 128 if d > 0 else 0
                r = i % 4
                gi = gcount[0]
                jo = gi % 2
                po, pob = ps[3 + jo], psb[3 + jo]
                P.add("pe", lambda e: e.matmul(
                    out=po[:, c0:512], lhsT=vp[j][:, kb, :], rhs=pt[r][:, c0:512],
                    start=(kb == 0), stop=(kb == nkb - 1)), reads=[vp_b[j], pt_b[r]], writes=[pob])
                if kb == nkb - 1:
                    gcount[0] += 1
                    P.add("dve", lambda e: e.tensor_copy(out=osb[jo][:], in_=po[0:64, :]),
                          reads=[pob], writes=[osb_b[jo]])
                    P.add("dve", lambda e: e.reciprocal(out=Rr[jo][64:128, :], in_=po[64:128, :]),
                          reads=[pob, Rr_b[jo]], writes=[Rr_b[jo]])
                    pending.append((i + LC, jo, h, seq * S + G * 512))

            def stageC(jo, h, cc):
                P.add("pe", lambda e: e.matmul(out=ps[5][:], lhsT=Sh[:], rhs=Rr[jo][:], start=True, stop=True),
                      reads=[Sh_b, Rr_b[jo]], writes=[psb[5]])
                P.add("dve", lambda e: e.tensor_tensor(out=yst[jo][:], in0=osb[jo][:], in1=ps[5][0:64, :],
                                                       op=ALU.mult),
                      reads=[osb_b[jo], psb[5]], writes=[yst_b[jo]])
                P.add("sp", lambda e: e.dma_start(out=self.YFT[h * 64:(h + 1) * 64, cc:cc + 512], in_=yst[jo][:]),
                      reads=[yst_b[jo]], writes=[self.YFT_b], dma_key=yst_b[jo].name)

            ssd = self.ssd_setup(ctx, 6, 7) if with_ssd else None
            load(heads[0][0], heads[0][1], 0)
            N = len(steps)
            for i in range(N + LA):
                if i < N:
                    stageA(i)
                if i - LA >= 0:
                    stageB(i - LA)
                while pending and pending[0][0] <= i:
                    _, jo, h, cc = pending.pop(0)
                    stageC(jo, h, cc)
                if ssd is not None and (i % 2 == 1 or i % 16 == 0):
                    if next(ssd, "done") == "done":
                        ssd = None
            while pending:
                _, jo, h, cc = pending.pop(0)
                stageC(jo, h, cc)
            if ssd is not None:
                for _ in ssd:
                    pass
            P.barrier()

    def ssd_setup(self, ctx, bx, by):
        nc, P = self.nc, self.P
        T, S = self.T, self.S
        tag = "p4"
        inp = self.inp
        ps, psb = self.ps, self.psb
        X, Xb, Y, Yb = ps[bx], psb[bx], ps[by], psb[by]

        def const(name, shape, dt=F32):
            return self.sb(ctx, tag + name, shape, dt), P.buf(tag + name)

        U, U_b = const("U", [128, 128])
        Tm, Tm_b = const("Tm", [128, 128])
        on, on_b = const("ones", [128, 128])
        P.add("pool", lambda e: e.memset(U[:], 1.0), writes=[U_b])
        P.add("pool", lambda e: e.affine_select(U[:], U[:], [[1, 128]], ALU.is_ge, 0.0, base=0,
                                                channel_multiplier=-1), reads=[U_b], writes=[U_b])
        P.add("pool", lambda e: e.memset(Tm[:], 1.0), writes=[Tm_b])
        P.add("pool", lambda e: e.affine_select(Tm[:], Tm[:], [[-1, 128]], ALU.is_gt, 0.0, base=0,
                                                channel_multiplier=1), reads=[Tm_b], writes=[Tm_b])
        P.add("pool", lambda e: e.memset(on[:], 1.0), writes=[on_b])
        dtb, dtb_b = const("dtb", [128, 16])
        At, At_b = const("At", [128, 16])
        Dsk, Dsk_b = const("Dsk", [128, 16])
        nw, nw_b = const("nw", [128, 1024])
        P.add("sp", lambda e: e.dma_start(out=dtb[:], in_=inp["dt_bias"].partition_broadcast(128)),
              reads=[self.wbuf], writes=[dtb_b], dma_key=dtb_b.name)
        P.add("sp", lambda e: e.dma_start(out=At[:], in_=inp["a_log"].partition_broadcast(128)),
              reads=[self.wbuf], writes=[At_b], dma_key=At_b.name)
        P.add("sp", lambda e: e.dma_start(out=Dsk[:], in_=inp["d_skip"].partition_broadcast(128)),
              reads=[self.wbuf], writes=[Dsk_b], dma_key=Dsk_b.name)
        P.add("sp", lambda e: e.dma_start(out=nw[:], in_=inp["ssd_norm_w"].partition_broadcast(128)),
              reads=[self.wbuf], writes=[nw_b], dma_key=nw_b.name)
        P.add("act", lambda e: e.activation(out=At[:], in_=At[:], func=AF.Exp), reads=[At_b], writes=[At_b])
        P.add("dve", lambda e: e.tensor_scalar(out=At[:], in0=At[:], scalar1=-1.0, scalar2=None, op0=ALU.mult),
              reads=[At_b], writes=[At_b])
        S32, S32_b = const("S32", [128, 2, 512])
        Sbf, Sbf_b = const("Sbf", [128, 2, 512], BF16)

        def dbl(name, shape, dt=F32):
            return ([self.sb(ctx, f"{tag}{name}{i}", shape, dt) for i in range(2)],
                    [P.buf(f"{tag}{name}{i}") for i in range(2)])

        xsb, xsb_b = dbl("xsb", [128, 1280], BF16)
        bct, bct_b = dbl("bct", [128, 4, 128], BF16)
        dtr, dtr_b = dbl("dtr", [128, 16])
        zs, zs_b = dbl("zs", [128, 1024], BF16)
        ystg, ystg_b = dbl("ystg", [128, 1024], BF16)
        dtv, dtv_b = dbl("dtv", [128, 16])
        av, av_b = dbl("av", [128, 16])
        cs, cs_b = dbl("cs", [128, 32])
        ex, ex_b = dbl("ex", [128, 32])
        dte, dte_b = dbl("dte", [128, 16])
        rhsA, rhsA_b = dbl("rhsA", [128, 8, 128])
        L, L_b = dbl("L", [128, 8, 128], BF16)
        cbm, cbm_b = dbl("cbm", [128, 128], BF16)
        M, M_b = dbl("M", [128, 8, 128], BF16)
        xdt, xdt_b = dbl("xdt", [128, 8, 64], BF16)
        xdd, xdd_b = dbl("xdd", [128, 8, 64], BF16)
        t1, t1_b = dbl("t1", [128, 512])
        t2, t2_b = dbl("t2", [128, 512])
        junk, junk_b = const("junk", [128, 512], BF16)
        ssy, ssy_b = dbl("ssy", [128, 1])
        rs, rs_b = dbl("rs", [128, 1])

        def load(tl, j):
            r0 = tl * 128
            P.add("sp", lambda e: e.dma_start(out=xsb[j][:], in_=self.XSB[r0:r0 + 128, :]),
                  reads=[self.XSB_b], writes=[xsb_b[j]], dma_key=xsb_b[j].name)
            P.add("sp", lambda e: e.dma_start(out=bct[j][:], in_=self.BCT[:, :, r0:r0 + 128].rearrange("c p t -> p c t")),
                  reads=[self.BCT_b], writes=[bct_b[j]], dma_key=bct_b[j].name)
            P.add("sp", lambda e: e.dma_start(out=dtr[j][:], in_=self.DT[r0:r0 + 128, :]),
                  reads=[self.DT_b], writes=[dtr_b[j]], dma_key=dtr_b[j].name)
            P.add("sp", lambda e: e.dma_start(out=zs[j][:], in_=self.ZS[r0:r0 + 128, :]),
                  reads=[self.ZS_b], writes=[zs_b[j]], dma_key=zs_b[j].name)

        def bc_h(ap16, g):
            return ap16[:, 8 * g:8 * g + 8].unsqueeze(2)

        NTL = T // 128

        def tile(tl):
            j = tl % 2
            r0 = tl * 128
            if tl + 1 < NTL:
                load(tl + 1, (tl + 1) % 2)
            if r0 % S == 0:
                P.add("pool", lambda e: e.memset(S32[:], 0.0), writes=[S32_b])
                P.add("pool", lambda e: e.memset(Sbf[:], 0.0), writes=[Sbf_b])
            P.add("dve", lambda e: e.tensor_tensor(out=dtv[j][:], in0=dtr[j][:], in1=dtb[:], op=ALU.add),
                  reads=[dtr_b[j], dtb_b], writes=[dtv_b[j]])
            yield
            P.add("act", lambda e: e.activation(out=dtv[j][:], in_=dtv[j][:], func=AF.Exp),
                  reads=[dtv_b[j]], writes=[dtv_b[j]])
            P.add("act", lambda e: e.activation(out=dtv[j][:], in_=dtv[j][:], func=AF.Ln, bias=1.0),
                  reads=[dtv_b[j]], writes=[dtv_b[j]])
            yield
            P.add("dve", lambda e: e.tensor_tensor(out=av[j][:], in0=dtv[j][:], in1=At[:], op=ALU.mult),
                  reads=[dtv_b[j], At_b], writes=[av_b[j]])
            yield
            P.add("pe", lambda e: e.matmul(out=X[:, 0:16], lhsT=U[:], rhs=av[j][:], start=True, stop=True),
                  reads=[U_b, av_b[j]], writes=[Xb])
            P.add("pe", lambda e: e.matmul(out=X[:, 16:32], lhsT=on[:], rhs=av[j][:], start=True, stop=True),
                  reads=[on_b, av_b[j]], writes=[Xb])
            yield
            P.add("dve", lambda e: e.tensor_copy(out=cs[j][:], in_=X[:, 0:32]), reads=[Xb], writes=[cs_b[j]])
            yield
            P.add("act", lambda e: e.activation(out=ex[j][:], in_=cs[j][:], func=AF.Exp),
                  reads=[cs_b[j]], writes=[ex_b[j]])
            P.add("dve", lambda e: e.tensor_tensor(out=dte[j][:], in0=cs[j][:, 16:32], in1=cs[j][:, 0:16],
                                                   op=ALU.subtract), reads=[cs_b[j]], writes=[dte_b[j]])
            yield
            P.add("act", lambda e: e.activation(out=dte[j][:], in_=dte[j][:], func=AF.Exp),
                  reads=[dte_b[j]], writes=[dte_b[j]])
            for g in range(2):
                gi = g
                xs_g = xsb[j][:, g * 512:(g + 1) * 512].rearrange("p (h d) -> p h d", h=8)
                P.add("dve", lambda e, g=g, gi=gi: e.tensor_tensor(
                    out=rhsA[gi][:], in0=bc_h(av[j], g).broadcast_to([128, 8, 128]),
                    in1=U[:].unsqueeze(1).broadcast_to([128, 8, 128]), op=ALU.mult),
                    reads=[av_b[j], U_b], writes=[rhsA_b[gi]])
                P.add("pool", lambda e, g=g, gi=gi, xs_g=xs_g: e.tensor_tensor(
                    out=xdt[gi][:], in0=xs_g, in1=bc_h(dtv[j], g).broadcast_to([128, 8, 64]), op=ALU.mult),
                    reads=[xsb_b[j], dtv_b[j]], writes=[xdt_b[gi]])
                yield
                for hf, (Pb, Pbb) in enumerate(((X, Xb), (Y, Yb))):
                    P.add("pe", lambda e, hf=hf, gi=gi, Pb=Pb: e.matmul(
                        out=Pb[:], lhsT=Tm[:],
                        rhs=rhsA[gi][:, 4 * hf:4 * hf + 4, :].rearrange("p h l -> p (h l)"),
                        start=True, stop=True), reads=[Tm_b, rhsA_b[gi]], writes=[Pbb])
                yield
                for hf, (Pb, Pbb) in enumerate(((X, Xb), (Y, Yb))):
                    P.add("act", lambda e, hf=hf, gi=gi, Pb=Pb: e.activation(
                        out=L[gi][:, 4 * hf:4 * hf + 4, :].rearrange("p h l -> p (h l)"), in_=Pb[:],
                        func=AF.Exp), reads=[Pbb], writes=[L_b[gi]])
                P.add("pool", lambda e, g=g, gi=gi: e.tensor_tensor(
                    out=xdd[gi][:], in0=xdt[gi][:], in1=bc_h(dte[j], g).broadcast_to([128, 8, 64]), op=ALU.mult),
                    reads=[xdt_b[gi], dte_b[j]], writes=[xdd_b[gi]])
                yield
                P.add("pe", lambda e, g=g: e.matmul(out=X[:, 0:128], lhsT=bct[j][:, g, :], rhs=bct[j][:, 2 + g, :],
                                                   start=True, stop=True), reads=[bct_b[j]], writes=[Xb])
                yield
                P.add("dve", lambda e, gi=gi: e.tensor_tensor(out=cbm[gi][:], in0=X[:, 0:128], in1=U[:], op=ALU.mult),
                      reads=[Xb, U_b], writes=[cbm_b[gi]])
                yield
                P.add("dve", lambda e, gi=gi: e.tensor_tensor(
                    out=M[gi][:], in0=L[gi][:], in1=cbm[gi][:].unsqueeze(1).broadcast_to([128, 8, 128]),
                    op=ALU.mult), reads=[L_b[gi], cbm_b[gi]], writes=[M_b[gi]])
                P.add("pool", lambda e, g=g, gi=gi, xs_g=xs_g: e.tensor_tensor(
                    out=t2[gi][:].rearrange("p (h d) -> p h d", h=8), in0=xs_g,
                    in1=bc_h(Dsk, g).broadcast_to([128, 8, 64]), op=ALU.mult),
                    reads=[xsb_b[j], Dsk_b], writes=[t2_b[gi]])
                yield
                for h in range(8):
                    P.add("pe", lambda e, h=h, gi=gi: e.matmul(
                        out=X[:, h * 64:(h + 1) * 64], lhsT=M[gi][:, h, :], rhs=xdt[gi][:, h, :],
                        start=True, stop=True), reads=[M_b[gi], xdt_b[gi]], writes=[Xb])
                P.add("pe", lambda e, g=g: e.matmul(out=Y[:], lhsT=bct[j][:, 2 + g, :], rhs=Sbf[:, g, :],
                                                   start=True, stop=True),
                      reads=[bct_b[j], Sbf_b], writes=[Yb])
                yield
                P.add("dve", lambda e, g=g, gi=gi: e.tensor_tensor(
                    out=t1[gi][:].rearrange("p (h d) -> p h d", h=8), in0=Y[:].rearrange("p (h d) -> p h d", h=8),
                    in1=bc_h(ex[j], g).broadcast_to([128, 8, 64]), op=ALU.mult),
                    reads=[Yb, ex_b[j]], writes=[t1_b[gi]])
                yield
                P.add("dve", lambda e, gi=gi: e.tensor_tensor(out=t1[gi][:], in0=t1[gi][:], in1=X[:], op=ALU.add),
                      reads=[t1_b[gi], Xb], writes=[t1_b[gi]])
                yield
                P.add("pe", lambda e, g=g, gi=gi: e.matmul(
                    out=X[:], lhsT=xsb[j][:, 1024 + g * 128:1024 + (g + 1) * 128],
                    rhs=xdd[gi][:].rearrange("p h d -> p (h d)"), start=True, stop=True),
                    reads=[xsb_b[j], xdd_b[gi]], writes=[Xb])
                P.add("dve", lambda e, gi=gi: e.tensor_tensor(out=t1[gi][:], in0=t1[gi][:], in1=t2[gi][:], op=ALU.add),
                      reads=[t1_b[gi], t2_b[gi]], writes=[t1_b[gi]])
                S32g = S32[:, g, :].rearrange("p (h d) -> p h d", h=8)
                P.add("pool", lambda e, g=g, S32g=S32g: e.tensor_tensor(
                    out=S32g, in0=S32g, in1=ex[j][:, 16 + 8 * g:16 + 8 * g + 8].unsqueeze(2).broadcast_to([128, 8, 64]),
                    op=ALU.mult), reads=[S32_b, ex_b[j]], writes=[S32_b])
                yield
                P.add("dve", lambda e, g=g, gi=gi: e.tensor_tensor(
                    out=t1[gi][:], in0=t1[gi][:], in1=zs[j][:, g * 512:(g + 1) * 512], op=ALU.mult),
                    reads=[t1_b[gi], zs_b[j]], writes=[t1_b[gi]])
                yield
                P.add("act", lambda e, gi=gi: e.activation(out=junk[:], in_=t1[gi][:], func=AF.Square,
                                                          accum_out=ssy[gi][:]),
                      reads=[t1_b[gi]], writes=[junk_b, ssy_b[gi]])
                P.add("dve", lambda e, g=g: e.tensor_tensor(out=S32[:, g, :], in0=S32[:, g, :], in1=X[:], op=ALU.add),
                      reads=[S32_b, Xb], writes=[S32_b])
                yield
                P.add("act", lambda e, gi=gi: e.activation(out=rs[gi][:], in_=ssy[gi][:], func=AF.Ln,
                                                          bias=self.eps_t[:, 0:1], scale=1.0 / 512),
                      reads=[ssy_b[gi], self.eps_b], writes=[rs_b[gi]])
                P.add("act", lambda e, g=g: e.copy(out=Sbf[:, g, :], in_=S32[:, g, :]),
                      reads=[S32_b], writes=[Sbf_b])
                yield
                P.add("act", lambda e, gi=gi: e.activation(out=rs[gi][:], in_=rs[gi][:], func=AF.Exp, scale=-0.5),
                      reads=[rs_b[gi]], writes=[rs_b[gi]])
                yield
                P.add("dve", lambda e, g=g, gi=gi: e.scalar_tensor_tensor(
                    out=ystg[j][:, g * 512:(g + 1) * 512], in0=t1[gi][:], scalar=rs[gi][:, 0:1],
                    in1=nw[:, g * 512:(g + 1) * 512], op0=ALU.mult, op1=ALU.mult),
                    reads=[t1_b[gi], rs_b[gi], nw_b], writes=[ystg_b[j]])
                yield
            P.add("sp", lambda e: e.dma_start(out=self.YS[r0:r0 + 128, :], in_=ystg[j][:]),
                  reads=[ystg_b[j]], writes=[self.YS_b], dma_key=ystg_b[j].name)

        def all_tiles():
            load(0, 0)
            for tl in range(NTL):
                yield from tile(tl)

        return all_tiles()

    def merge_phase(self):
        nc, P = self.nc, self.P
        T, S = self.T, self.S
        NT = T // 512
        tag = "p5"
        inp = self.inp
        ps, psb = self.ps, self.psb
        with contextlib.ExitStack() as ctx:
            def wload(name, src, c0, ncol):
                w = self.sb(ctx, tag + name, [128, 8, ncol], BF16)
                w_b = P.buf(tag + name, multi=True)
                v = src.rearrange("(k p) f -> p k f", p=128)
                for k in range(8):
                    P.add("pool", lambda e, k=k: e.dma_start(out=w[:, k, :], in_=v[:, k, c0:c0 + ncol],
                                                              max_dma_last_dim=4096),
                          reads=[self.wbuf], writes=[w_b], dma_key=w_b.name)
                return w, w_b

            wbs, wbs_b = wload("wbs", inp["w_branch_ssd"], 0, 1024)
            wg, wg_b = wload("wg", inp["w_in"], 5664, 2048)
            wbf, wbf_b = wload("wbf", inp["w_branch_fox"], 0, 1024)
            wo, wo_b = wload("wo", inp["w_out"], 0, 1024)
            hTs = [self.sb(ctx, f"{tag}hT{i}", [128, 8, 512], BF16) for i in range(2)]
            hTs_b = [P.buf(f"{tag}hT{i}") for i in range(2)]
            yst4 = [self.sb(ctx, f"{tag}yst4{i}", [128, 4, 1024], BF16) for i in range(2)]
            yst4_b = [P.buf(f"{tag}yst4{i}") for i in range(2)]
            yfT = [self.sb(ctx, f"{tag}yfT{i}", [128, 8, 512], BF16) for i in range(2)]
            yfT_b = [P.buf(f"{tag}yfT{i}") for i in range(2)]
            ysT = self.sb(ctx, tag + "ysT", [128, 8, 512], BF16)
            ysT_b = P.buf(tag + "ysT")
            mT = self.sb(ctx, tag + "mT", [128, 8, 512], BF16)
            mT_b = P.buf(tag + "mT")
            sg = [self.sb(ctx, f"{tag}sg{i}", [128, 512], F32) for i in range(2)]
            sg_b = [P.buf(f"{tag}sg{i}") for i in range(2)]
            tm = [self.sb(ctx, f"{tag}tm{i}", [128, 512], F32) for i in range(2)]
            tm_b = [P.buf(f"{tag}tm{i}") for i in range(2)]
            xr = [self.sb(ctx, f"{tag}xr{i}", [128, 512], F32) for i in range(2)]
            xr_b = [P.buf(f"{tag}xr{i}") for i in range(2)]

            def loads(it, j):
                r0 = it * 512
                P.add("sp", lambda e: e.dma_start(
                    out=yst4[j][:], in_=self.YS[r0:r0 + 512, :].rearrange("(s p) d -> p s d", p=128)),
                    reads=[self.YS_b], writes=[yst4_b[j]], dma_key=yst4_b[j].name)
                P.add("sp", lambda e: e.dma_start(
                    out=yfT[j][:], in_=self.YFT[:, r0:r0 + 512].rearrange("(k p) t -> p k t", p=128)),
                    reads=[self.YFT_b], writes=[yfT_b[j]], dma_key=yfT_b[j].name)
                P.add("sp", lambda e: e.dma_start(
                    out=hTs[j][:], in_=self.HT[:, r0:r0 + 512].rearrange("(k p) t -> p k t", p=128)),
                    reads=[self.HT_b], writes=[hTs_b[j]], dma_key=hTs_b[j].name)

            def ys_transposes(j):
                self.fe_b({"hn4": yst4[j], "hn4_b": yst4_b[j], "hT": ysT, "hT_b": ysT_b}, (6, 7))

            n2 = [0]
            nr = [0]

            def body(it, j):
                r0 = it * 512
                hT, hb = hTs[j], hTs_b[j]
                for c in range(8):
                    for part in range(2):
                        jj = n2[0] % 2
                        n2[0] += 1
                        pA, pAb = ps[2 * jj], psb[2 * jj]
                        pG, pGb = ps[2 * jj + 1], psb[2 * jj + 1]
                        wbr, wbr_b = (wbs, wbs_b) if part == 0 else (wbf, wbf_b)
                        yT, yT_b = (ysT, ysT_b) if part == 0 else (yfT[j], yfT_b[j])
                        for k in range(8):
                            P.add("pe", lambda e, k=k, c=c, pA=pA, wbr=wbr, yT=yT: e.matmul(
                                out=pA[:], lhsT=wbr[:, k, c * 128:(c + 1) * 128], rhs=yT[:, k, :],
                                start=(k == 0), stop=(k == 7)), reads=[wbr_b, yT_b], writes=[pAb])
                        gc = part * 1024 + c * 128
                        for k in range(8):
                            P.add("pe", lambda e, k=k, gc=gc, pG=pG: e.matmul(
                                out=pG[:], lhsT=wg[:, k, gc:gc + 128], rhs=hT[:, k, :],
                                start=(k == 0), stop=(k == 7)), reads=[wg_b, hb], writes=[pGb])
                        P.add("act", lambda e, pG=pG, part=part: e.activation(out=sg[part][:], in_=pG[:], func=AF.Sigmoid),
                              reads=[pGb], writes=[sg_b[part]])
                        P.add("dve", lambda e, pA=pA, part=part: e.tensor_tensor(
                            out=tm[part][:], in0=sg[part][:], in1=pA[:], op=ALU.mult),
                            reads=[sg_b[part], pAb], writes=[tm_b[part]])
                    P.add("dve", lambda e, c=c: e.tensor_tensor(out=mT[:, c, :], in0=tm[0][:], in1=tm[1][:], op=ALU.add),
                          reads=[tm_b[0], tm_b[1]], writes=[mT_b])
                for sub in range(4):
                    for half in range(2):
                        jr = nr[0] % 2
                        nr[0] += 1
                        pc, pcb = ps[4 + jr], psb[4 + jr]
                        rr = r0 + sub * 128
                        P.add("sp", lambda e, jr=jr, rr=rr, half=half: e.dma_start(
                            out=xr[jr][:], in_=self.X1[rr:rr + 128, half * 512:(half + 1) * 512]),
                            reads=[self.X1_b], writes=[xr_b[jr]], dma_key=xr_b[jr].name)
                        for k in range(8):
                            P.add("pe", lambda e, k=k, pc=pc, sub=sub, half=half: e.matmul(
                                out=pc[:], lhsT=mT[:, k, sub * 128:(sub + 1) * 128],
                                rhs=wo[:, k, half * 512:(half + 1) * 512],
                                start=(k == 0), stop=(k == 7)), reads=[wo_b, mT_b], writes=[pcb])
                        P.add("dve", lambda e, jr=jr, pc=pc: e.tensor_tensor(
                            out=xr[jr][:], in0=pc[:], in1=xr[jr][:], op=ALU.add),
                            reads=[pcb, xr_b[jr]], writes=[xr_b[jr]])
                        P.add("sp", lambda e, jr=jr, rr=rr, half=half: e.dma_start(
                            out=self.X2[rr:rr + 128, half * 512:(half + 1) * 512], in_=xr[jr][:]),
                            reads=[xr_b[jr]], writes=[self.X2_b], dma_key=xr_b[jr].name)

            loads(0, 0)
            ys_transposes(0)
            for it in range(NT):
                j = it % 2
                if it + 1 < NT:
                    loads(it + 1, (it + 1) % 2)
                body(it, j)
                if it + 1 < NT:
                    ys_transposes((it + 1) % 2)
            P.barrier()


_CACHE = {}


def _get_nc(S, stages="all", debug_outs=()):
    key = (S, stages, tuple(debug_outs))
    if key not in _CACHE:
        b = Builder(S, stages, debug_outs)
        nc = b.build()
        _CACHE[key] = (nc, b)
    return _CACHE[key]


def run(inputs, S, stages="all", debug_outs=(), trace=False):
    nc, b = _get_nc(S, stages, debug_outs)
    x = np.ascontiguousarray(inputs["x"], dtype=np.float32)
    B = x.shape[0]
    assert B == 2 * NCORES and x.shape[1] == S
    in_maps = []
    for c in range(NCORES):
        m = {"x": np.ascontiguousarray(x[2 * c:2 * c + 2].reshape(2 * S, D))}
        for n in b.inp:
            a = np.asarray(inputs[n], dtype=np.float32)
            m[n] = np.ascontiguousarray(a[0].reshape(tuple(b.inp[n].shape)))
        in_maps.append(m)
    res = run_bass_kernel_spmd(nc, in_maps, core_ids=list(range(NCORES)), trace=trace)
    out = np.concatenate([r["out"].reshape(2, S, D) for r in res.results], axis=0)
    return out, res


def kernel(**inputs):
    out, _ = run(inputs, 4096)
    return out.astype(np.float32)
```
